# Optimizing a Trainium2 kernel written in Bass

```python
import math
import jax, jax.numpy as jnp
from jax import lax
import numpy as np

D_MODEL = 1024
BATCH = 16
SEQ = 256
DEPTH = 4
DEC_BATCH = 4
DEC_SEQ = 2048
PAST_LEN = 256

GRID_W = 64
MIX_WIDTH = D_MODEL
GROUP_W = MIX_WIDTH // 4
A_HEAD_DIM = 64
A_Q_HEADS = GROUP_W // A_HEAD_DIM
A_KV_HEADS = A_Q_HEADS // 2
B_HEADS = 4
B_V_DIM = GROUP_W // B_HEADS
B_QK_DIM = B_V_DIM // 2
POOL_WINDOWS = (2, 4, 8, 16)
C_GROUPS = len(POOL_WINDOWS)
C_GROUP_DIM = GROUP_W // C_GROUPS
CHUNK = 128
D_GROUPS = 4
D_GROUP_DIM = GROUP_W // D_GROUPS
D_FF = -(-8 * D_MODEL // (3 * 256)) * 256
N_MOD = 6
QBLOCK = 128
RMS_EPS = 1e-6
ROPE_THETA = 10000.0

IN_SIZES = [A_Q_HEADS * A_HEAD_DIM, A_KV_HEADS * A_HEAD_DIM, A_KV_HEADS * A_HEAD_DIM,
            B_HEADS * 2 * B_QK_DIM, B_HEADS * 2 * B_QK_DIM, B_HEADS * B_V_DIM,
            GROUP_W, 2 * GROUP_W]
IN_WIDTH = sum(IN_SIZES)
IN_SPLITS = [int(v) for v in np.cumsum(IN_SIZES)[:-1]]

kernel_name = "hybrid_diffusion_prefix_step"


def rms_norm(x, g):
    xf = x.astype(jnp.float32)
    y = xf * lax.rsqrt(jnp.mean(xf * xf, axis=-1, keepdims=True) + RMS_EPS)
    return (y * g.astype(jnp.float32)).astype(x.dtype)


def axial_rope(x):
    s, d = x.shape[1], x.shape[-1]
    rows = s // GRID_W
    row = jnp.repeat(jnp.arange(rows), GRID_W)
    col = jnp.tile(jnp.arange(GRID_W), rows)
    half = d // 2
    inv_freq = ROPE_THETA ** (-jnp.arange(0, half, 2, dtype=jnp.float32) / half)

    def rotate(xp, pos):
        ang = pos.astype(jnp.float32)[:, None] * inv_freq[None, :]
        ang = jnp.concatenate([ang, ang], axis=-1)[None, :, None, :]
        xf = xp.astype(jnp.float32)
        x1, x2 = jnp.split(xf, 2, axis=-1)
        return xf * jnp.cos(ang) + jnp.concatenate([-x2, x1], axis=-1) * jnp.sin(ang)

    out = jnp.concatenate([rotate(x[..., :half], row), rotate(x[..., half:], col)], axis=-1)
    return out.astype(x.dtype)


def gqa_attention(q, k, v):
    b, sq, hq, d = q.shape
    hkv = k.shape[2]
    g = hq // hkv
    nb = sq // QBLOCK
    scale = d ** -0.5
    qb = jnp.moveaxis(q.reshape(b, nb, QBLOCK, hkv, g, d), 1, 0)

    def block(qi):
        s = jnp.einsum("bqhgd,bkhd->bhgqk", qi, k, preferred_element_type=jnp.float32) * scale
        p = jax.nn.softmax(s, axis=-1)
        return jnp.einsum("bhgqk,bkhd->bqhgd", p.astype(v.dtype), v)

    o = jnp.moveaxis(lax.map(block, qb), 0, 1)
    return o.reshape(b, sq, hq * d)


def diff_attention(q, k, v, lam, lam_init, g_subln):
    b, sq, h, _, d = q.shape
    e = v.shape[-1]
    nb = sq // QBLOCK
    scale = d ** -0.5
    qb = jnp.moveaxis(q.reshape(b, nb, QBLOCK, h, 2, d), 1, 0)

    def block(qi):
        s = jnp.einsum("bqhjd,bkhjd->bhjqk", qi, k, preferred_element_type=jnp.float32) * scale
        p = jax.nn.softmax(s, axis=-1)
        w = p[:, :, 0] - lam * p[:, :, 1]
        return jnp.einsum("bhqk,bkhe->bqhe", w.astype(v.dtype), v)

    o = jnp.moveaxis(lax.map(block, qb), 0, 1).reshape(b, sq, h, e)
    o = rms_norm(o, g_subln) * (1.0 - lam_init)
    return o.reshape(b, sq, h * e)


def multiscale_pool(x, w_c, c_scale):
    b, s, _ = x.shape
    xg = x.reshape(b, s, C_GROUPS, C_GROUP_DIM)
    xf = xg.astype(jnp.float32)
    csum = jnp.concatenate([jnp.zeros((b, 1, C_GROUPS, C_GROUP_DIM), jnp.float32),
                            jnp.cumsum(xf, axis=1)], axis=1)
    t = jnp.arange(s)
    outs = []
    for gi, win in enumerate(POOL_WINDOWS):
        lo = jnp.clip(t - win // 2, 0, s)
        hi = jnp.clip(t + win // 2, 0, s)
        cs = csum[:, :, gi]
        mean = (cs[:, hi] - cs[:, lo]) / (hi - lo).astype(jnp.float32)[None, :, None]
        outs.append(mean - xf[:, :, gi])
    pooled = jnp.stack(outs, axis=2).astype(x.dtype)
    y = jnp.einsum("bsgc,gce->bsge", pooled, w_c).reshape(b, s, GROUP_W)
    return y * c_scale


def spatial_gating(z, g_v, w_s, b_s):
    b, s, _ = z.shape
    z = jax.nn.gelu(z)
    u, v = jnp.split(z, 2, axis=-1)
    v = rms_norm(v, g_v)
    vg = v.reshape(b, s // CHUNK, CHUNK, D_GROUPS, D_GROUP_DIM)
    sg = jnp.einsum("gpq,bnqgc->bnpgc", w_s, vg) + b_s.T[None, None, :, :, None]
    return u * sg.reshape(b, s, GROUP_W)


def modulation(cond, w_mod, b_mod):
    m = jax.nn.silu(cond) @ w_mod + b_mod
    return m.reshape(cond.shape[:-1] + (N_MOD, D_MODEL))


def mixer_block(h, p, lam_init, ctx):
    b, s, _ = h.shape
    proj = h @ p["w_in"]
    aq, ak, av, bq, bk, bv, cx, dz = jnp.split(proj, IN_SPLITS, axis=-1)
    aq = rms_norm(aq.reshape(b, s, A_Q_HEADS, A_HEAD_DIM), p["a_q_norm"])
    ak = rms_norm(ak.reshape(b, s, A_KV_HEADS, A_HEAD_DIM), p["a_k_norm"])
    av = av.reshape(b, s, A_KV_HEADS, A_HEAD_DIM)
    bq = bq.reshape(b, s, B_HEADS, 2, B_QK_DIM)
    bk = bk.reshape(b, s, B_HEADS, 2, B_QK_DIM)
    bv = bv.reshape(b, s, B_HEADS, B_V_DIM)
    own = (ak, av, bk, bv)
    if ctx is not None:
        aq = axial_rope(aq)
        ak = jnp.concatenate([ctx[0], axial_rope(ak)], axis=1)
        av = jnp.concatenate([ctx[1], av], axis=1)
        bq = axial_rope(bq.reshape(b, s, B_HEADS * 2, B_QK_DIM)).reshape(b, s, B_HEADS, 2, B_QK_DIM)
        bk_lat = axial_rope(bk.reshape(b, s, B_HEADS * 2, B_QK_DIM)).reshape(b, s, B_HEADS, 2, B_QK_DIM)
        bk = jnp.concatenate([ctx[2], bk_lat], axis=1)
        bv = jnp.concatenate([ctx[3], bv], axis=1)
    a_out = gqa_attention(aq, ak, av)
    f32 = jnp.float32
    lam = (jnp.exp(jnp.sum(p["b_lq1"].astype(f32) * p["b_lk1"].astype(f32)))
           - jnp.exp(jnp.sum(p["b_lq2"].astype(f32) * p["b_lk2"].astype(f32))) + lam_init)
    b_out = diff_attention(bq, bk, bv, lam, lam_init, p["b_subln"])
    c_out = multiscale_pool(cx, p["c_w"], p["c_scale"])
    d_out = spatial_gating(dz, p["d_v_norm"], p["d_ws"], p["d_bs"])
    y = jnp.concatenate([a_out.astype(h.dtype), b_out.astype(h.dtype),
                         c_out.astype(h.dtype), d_out.astype(h.dtype)], axis=-1) @ p["w_out"]
    return y, own


def trunk_layer(x, mod, p, lam_init, ctx):
    shift1, scale1, gate1, shift2, scale2, gate2 = [mod[:, i][:, None] for i in range(N_MOD)]
    h = rms_norm(x, p["g_pre_mix"]) * (1.0 + scale1) + shift1
    m, own = mixer_block(h, p, lam_init, ctx)
    x = x + gate1 * rms_norm(m, p["g_post_mix"])
    h = rms_norm(x, p["g_pre_ffn"]) * (1.0 + scale2) + shift2
    f = (jax.nn.silu(h @ p["w_gate"]) * (h @ p["w_up"])) @ p["w_down"]
    x = x + gate2 * rms_norm(f, p["g_post_ffn"])
    return x, own


def setup_inputs(seed: int = 0) -> dict:
    key = jax.random.key(seed)
    ks = jax.random.split(key, 32)
    f32 = jnp.float32

    def nrm(k, shape, s=1.0):
        return jax.random.normal(k, shape, f32) * s

    return {
        "x_prompt": nrm(ks[0], (BATCH, SEQ, D_MODEL)),
        "x_sample": nrm(ks[1], (DEC_BATCH, DEC_SEQ, D_MODEL)),
        "cache_a_k": nrm(ks[2], (DEC_BATCH, DEPTH, PAST_LEN, A_KV_HEADS, A_HEAD_DIM)),
        "cache_a_v": nrm(ks[3], (DEC_BATCH, DEPTH, PAST_LEN, A_KV_HEADS, A_HEAD_DIM)),
        "cache_b_k": nrm(ks[4], (DEC_BATCH, DEPTH, PAST_LEN, B_HEADS, 2, B_QK_DIM)),
        "cache_b_v": nrm(ks[5], (DEC_BATCH, DEPTH, PAST_LEN, B_HEADS, B_V_DIM)),
        "c": nrm(ks[6], (DEC_BATCH, D_MODEL)),
        "c_ctx": nrm(ks[7], (D_MODEL,)),
        "w_mod": nrm(ks[8], (DEPTH, D_MODEL, N_MOD * D_MODEL), 0.5 * D_MODEL ** -0.5),
        "b_mod": nrm(ks[9], (DEPTH, N_MOD * D_MODEL), 0.02),
        "g_pre_mix": 1.0 + nrm(ks[10], (DEPTH, D_MODEL), 0.05),
        "g_post_mix": 1.0 + nrm(ks[11], (DEPTH, D_MODEL), 0.05),
        "g_pre_ffn": 1.0 + nrm(ks[12], (DEPTH, D_MODEL), 0.05),
        "g_post_ffn": 1.0 + nrm(ks[13], (DEPTH, D_MODEL), 0.05),
        "w_in": nrm(ks[14], (DEPTH, D_MODEL, IN_WIDTH), D_MODEL ** -0.5),
        "w_out": nrm(ks[15], (DEPTH, MIX_WIDTH, D_MODEL), MIX_WIDTH ** -0.5),
        "a_q_norm": 1.0 + nrm(ks[16], (DEPTH, A_HEAD_DIM), 0.05),
        "a_k_norm": 1.0 + nrm(ks[17], (DEPTH, A_HEAD_DIM), 0.05),
        "b_lq1": nrm(ks[18], (DEPTH, B_QK_DIM), 0.1),
        "b_lk1": nrm(ks[19], (DEPTH, B_QK_DIM), 0.1),
        "b_lq2": nrm(ks[20], (DEPTH, B_QK_DIM), 0.1),
        "b_lk2": nrm(ks[21], (DEPTH, B_QK_DIM), 0.1),
        "b_subln": 1.0 + nrm(ks[22], (DEPTH, B_V_DIM), 0.05),
        "c_w": nrm(ks[23], (DEPTH, C_GROUPS, C_GROUP_DIM, C_GROUP_DIM), C_GROUP_DIM ** -0.5),
        "c_scale": 1.0 + nrm(ks[24], (DEPTH, GROUP_W), 0.1),
        "d_v_norm": 1.0 + nrm(ks[25], (DEPTH, GROUP_W), 0.05),
        "d_ws": nrm(ks[26], (DEPTH, D_GROUPS, CHUNK, CHUNK), CHUNK ** -0.5),
        "d_bs": 1.0 + nrm(ks[27], (DEPTH, D_GROUPS, CHUNK), 0.1),
        "w_gate": nrm(ks[28], (DEPTH, D_MODEL, D_FF), D_MODEL ** -0.5),
        "w_up": nrm(ks[29], (DEPTH, D_MODEL, D_FF), D_MODEL ** -0.5),
        "w_down": nrm(ks[30], (DEPTH, D_FF, D_MODEL), D_FF ** -0.5),
    }


def reference(x_prompt, x_sample, cache_a_k, cache_a_v, cache_b_k, cache_b_v, c, c_ctx,
              w_mod, b_mod, g_pre_mix, g_post_mix, g_pre_ffn, g_post_ffn, w_in, w_out,
              a_q_norm, a_k_norm, b_lq1, b_lk1, b_lq2, b_lk2, b_subln, c_w, c_scale,
              d_v_norm, d_ws, d_bs, w_gate, w_up, w_down):
    yp = x_prompt
    ys = x_sample
    new_ak, new_av, new_bk, new_bv = [], [], [], []
    for layer in range(DEPTH):
        p = {
            "w_in": w_in[layer], "w_out": w_out[layer],
            "g_pre_mix": g_pre_mix[layer], "g_post_mix": g_post_mix[layer],
            "g_pre_ffn": g_pre_ffn[layer], "g_post_ffn": g_post_ffn[layer],
            "a_q_norm": a_q_norm[layer], "a_k_norm": a_k_norm[layer],
            "b_lq1": b_lq1[layer], "b_lk1": b_lk1[layer],
            "b_lq2": b_lq2[layer], "b_lk2": b_lk2[layer], "b_subln": b_subln[layer],
            "c_w": c_w[layer], "c_scale": c_scale[layer],
            "d_v_norm": d_v_norm[layer], "d_ws": d_ws[layer], "d_bs": d_bs[layer],
            "w_gate": w_gate[layer], "w_up": w_up[layer], "w_down": w_down[layer],
        }
        lam_init = 0.8 - 0.6 * math.exp(-0.3 * layer)
        mod_ctx = modulation(c_ctx, w_mod[layer], b_mod[layer])[None]
        yp, own = trunk_layer(yp, mod_ctx, p, lam_init, None)
        new_ak.append(own[0])
        new_av.append(own[1])
        new_bk.append(own[2])
        new_bv.append(own[3])
        mod_lat = modulation(c, w_mod[layer], b_mod[layer])
        ctx = (cache_a_k[:, layer], cache_a_v[:, layer], cache_b_k[:, layer], cache_b_v[:, layer])
        ys, _ = trunk_layer(ys, mod_lat, p, lam_init, ctx)
    new_a_k = jnp.stack(new_ak, axis=1)
    new_a_v = jnp.stack(new_av, axis=1)
    new_b_k = jnp.stack(new_bk, axis=1)
    new_b_v = jnp.stack(new_bv, axis=1)
    return (yp, ys, new_a_k, new_a_v, new_b_k, new_b_v)
```

```python
import numpy as np
import concourse.bass as bass
import concourse.mybir as mybir

F32 = mybir.dt.float32
BF16 = mybir.dt.bfloat16
AF = mybir.ActivationFunctionType
ALU = mybir.AluOpType


class Res:
    __slots__ = ("name", "w", "r")

    def __init__(self, name):
        self.name = name
        self.w = None
        self.r = []


class Node:
    __slots__ = ("eng", "fn", "deps", "signal", "val", "chan", "idx")

    def __init__(self, eng, fn, chan=None):
        self.eng = eng
        self.fn = fn
        self.deps = []
        self.signal = False
        self.val = None
        self.chan = chan
        self.idx = None


class Chan:
    def __init__(self, name):
        self.name = name
        self.sem = None
        self.nodes = []


class FW:
    ENGS = ("pe", "act", "dve", "pool", "sp")

    def __init__(self):
        self.streams = {e: [] for e in self.ENGS}
        self.chans = []
        self.inorder = {"pe"}
        self.last = {}
        self.open_dma = []
        self.bar = []
        self.bar_pending = set()

    def chan(self, name):
        c = Chan(name)
        self.chans.append(c)
        return c

    def barrier(self):
        self.bar = [n for n in self.last.values()] + list(self.open_dma)
        self.open_dma = []
        self.bar_pending = set(self.ENGS)

    def _track(self, node, reads, writes):
        deps = node.deps
        if node.eng in self.bar_pending:
            deps.extend(self.bar)
            self.bar_pending.discard(node.eng)
        self.last[node.eng] = node
        for r in reads:
            if r.w is not None:
                deps.append(r.w)
            r.r.append(node)
        for w in writes:
            if w.w is not None:
                deps.append(w.w)
            deps.extend(w.r)
            w.w = node
            w.r = []

    def op(self, eng, fn, reads=(), writes=()):
        n = Node(eng, fn)
        n.idx = len(self.streams[eng])
        self._track(n, reads, writes)
        self.streams[eng].append(n)
        return n

    def dma(self, eng, chan, fn, reads=(), writes=()):
        n = Node(eng, fn, chan=chan)
        n.idx = len(self.streams[eng])
        n.signal = True
        if chan.nodes:
            n.deps.append(chan.nodes[-1])
        self.open_dma.append(n)
        chan.nodes.append(n)
        n.val = 16 * len(chan.nodes)
        self._track(n, reads, writes)
        self.streams[eng].append(n)
        return n

    def emit(self, nc, stack, final_wait_eng="sp"):
        for e in self.ENGS:
            for n in self.streams[e]:
                for d in n.deps:
                    if d is n:
                        continue
                    if d.chan is None:
                        if d.eng == n.eng and n.chan is None and d.eng in self.inorder:
                            continue
                        d.signal = True
        esem = {}
        for e in self.ENGS:
            cnt = 0
            for n in self.streams[e]:
                if n.chan is None and n.signal:
                    cnt += 1
                    n.val = cnt
            if cnt > 0:
                esem[e] = stack.enter_context(nc.semaphore("sem_" + e))
        for c in self.chans:
            if c.nodes:
                c.sem = stack.enter_context(nc.semaphore("ch_" + c.name))
        handles = {"pe": nc.tensor, "act": nc.scalar, "dve": nc.vector,
                   "pool": nc.gpsimd, "sp": nc.sync}
        stats = {}
        block = stack.enter_context(nc.Block())

        def emit_stream(e, h):
            known = {}
            nw = 0
            for n in self.streams[e]:
                need = {}
                for d in n.deps:
                    if d is n:
                        continue
                    if d.chan is not None:
                        key = ("c", id(d.chan)); sem = d.chan.sem
                    else:
                        if d.eng == n.eng and n.chan is None and d.eng in self.inorder:
                            continue
                        key = ("e", d.eng); sem = esem[d.eng]
                    if d.val > known.get(key, 0) and d.val > need.get(key, (None, 0))[1]:
                        need[key] = (sem, d.val)
                for key, (sem, v) in need.items():
                    h.wait_ge(sem, v)
                    known[key] = v
                    nw += 1
                inst = n.fn(h)
                if n.chan is not None:
                    inst.then_inc(n.chan.sem, 16)
                elif n.signal:
                    inst.then_inc(esem[e], 1)
            if e == final_wait_eng:
                for c in self.chans:
                    if c.nodes:
                        v = 16 * len(c.nodes)
                        if v > known.get(("c", id(c)), 0):
                            h.wait_ge(c.sem, v)
                for e2, s in esem.items():
                    last = max((n.val for n in self.streams[e2] if n.chan is None and n.signal), default=0)
                    if last > known.get(("e", e2), 0) and e2 != e:
                        h.wait_ge(s, last)
            stats[e] = (len(self.streams[e]), nw)

        @block.sync
        def _(sync):
            emit_stream("sp", sync)

        @block.tensor
        def _(t):
            emit_stream("pe", t)

        @block.scalar
        def _(s):
            emit_stream("act", s)

        @block.vector
        def _(v):
            emit_stream("dve", v)

        @block.gpsimd
        def _(g):
            emit_stream("pool", g)

        return stats


import math
from contextlib import ExitStack
from concourse.bass_utils import run_bass_kernel_spmd

L = 4; D = 1024; T = 2560; NT = 20; DFF = 2816; NJ = 22
SEQS = [(0, 2, 0, False), (2, 2, 0, False), (4, 16, 256, True)]
KT0 = [0, 2, 4]
NKT = 22
EPS = 1e-6
LAM_INIT = [0.8 - 0.6 * math.exp(-0.3 * l) for l in range(L)]


def _prod(s):
    r = 1
    for v in s:
        r *= v
    return r


def build():
    nc = bass.Bass("TRN2", target_bir_lowering=False)
    fw = FW()
    st = ExitStack()

    def din(name, shape, dt=F32):
        return nc.dram_tensor(name, list(shape), dt, kind="ExternalInput").ap()

    def dout(name, shape):
        return nc.dram_tensor(name, list(shape), F32, kind="ExternalOutput").ap()

    def dscr(name, shape, dt):
        return nc.dram_tensor(name, list(shape), dt, kind="Internal").ap()

    xin = din("xin", [T, D]); cak = din("cak", [L, 256, 128]); cav = din("cav", [L, 256, 128])
    cbk = din("cbk", [L, 256, 256]); cbv = din("cbv", [L, 256, 256]); cond = din("cond", [2, D])
    w_mod = din("w_mod", [L, D, 6 * D]); b_mod = din("b_mod", [L, 6 * D])
    gpre = din("gpre", [L, 128, 2, 8]); gpost = din("gpost", [L, 2, D])
    w_in = din("w_in", [L, D, 2048]); w_out = din("w_out", [L, D, D])
    gqk = din("gqk", [L, 384]); lqk = din("lqk", [4, L * 32]); gsub = din("gsub", [L, 256])
    c_w = din("c_w", [L, 4, 64, 64]); csc = din("csc", [L, 256]); gv = din("gv", [L, 256])
    wst = din("wst", [L, 128, 4, 128]); bse = din("bse", [L, 128, 256])
    w_gate = din("w_gate", [L, D, DFF]); w_up = din("w_up", [L, D, DFF]); w_down = din("w_down", [L, DFF, D])
    rope = din("rope", [16, 128, 192]); btm = din("btm", [128, 20, 128])
    y = dout("y", [T, D]); nak = dout("nak", [L, 512, 128]); nav = dout("nav", [L, 512, 128])
    nbk = dout("nbk", [L, 512, 256]); nbv = dout("nbv", [L, 512, 256])

    XR = dscr("XR", [T, D], F32); MODS = dscr("MODS", [L, 2, 6 * D], F32)
    QTA = dscr("QTA", [128, 2, T], BF16); KTA = dscr("KTA", [128, NKT * 128], BF16)
    VA = dscr("VA", [NKT * 128, 128], BF16)
    QTB = dscr("QTB", [128, 2, T], BF16); KTB = dscr("KTB", [128, 2, NKT * 128], BF16)
    VB = dscr("VB", [NKT * 128, 256], BF16)
    CX = dscr("CX", [T, 256], BF16); MIX = dscr("MIX", [T, D], BF16)
    RS = {}
    def rs(name, i):
        k = (name, i)
        if k not in RS:
            RS[k] = Res("%s%d" % (name, i))
        return RS[k]

    def sb(name, shape, dt):
        return st.enter_context(nc.sbuf_tensor(name, list(shape), dt))
    ARENA_B = 165 * 1024
    arena_t = sb("arena", [128, ARENA_B // 2], BF16)
    aoff = [0]
    def areset():
        aoff[0] = 0
        chi[0] = 0
    def aget(shape, dt):
        n = _prod(shape); nb = n * (2 if dt == BF16 else 4)
        off = (aoff[0] + 63) // 64 * 64
        aoff[0] = off + nb
        assert aoff[0] <= ARENA_B, (aoff[0], ARENA_B)
        ap = arena_t[:, off // 2:(off + nb) // 2]
        if dt == F32:
            ap = ap.bitcast(F32)
        if len(shape) > 1:
            names = "abcd"[:len(shape)]
            ap = ap.rearrange("p (%s) -> p %s" % (" ".join(names), " ".join(names)),
                              **{names[i]: shape[i] for i in range(len(shape))})
        return ap
    cnt = [0]
    chpool = []; chi = [0]; persist = [True]
    def getch():
        if persist[0]:
            return fw.chan("pc%d" % len(fw.chans))
        if chi[0] >= len(chpool):
            chpool.append(fw.chan("sc%d" % len(chpool)))
        c = chpool[chi[0]]; chi[0] += 1
        return c
    class Buf:
        def __init__(self, ap, name=None, dma=False):
            cnt[0] += 1
            self.ap = ap; self.res = Res(name or ("b%d" % cnt[0]))
            self.ch = getch() if dma else None
    class Rot:
        def __init__(self, n, shape, dt, dma=False, arena=True, name="r"):
            self.b = []
            for i in range(n):
                ap = aget(shape, dt) if arena else sb("%s%d_%d" % (name, cnt[0], i), [128] + list(shape), dt)
                self.b.append(Buf(ap, dma=dma))
            self.i = 0
        def next(self):
            b = self.b[self.i % len(self.b)]; self.i += 1
            return b

    ident = Buf(sb("ident", [128, 128], F32)); identb = Buf(sb("identb", [128, 128], BF16))
    onesb = Buf(sb("onesb", [128, 1], F32))
    scT = Buf(sb("scT", [128, 8, 2], F32), dma=True)
    modp = Buf(sb("modp", [128, 2, 6, 8], F32), dma=True)
    gpre_sb = Buf(sb("gpre_sb", [128, 2, 8], F32), dma=True)
    gms = Buf(sb("gms", [128, 2, 2, 8], F32))
    gg = Buf(sb("gg", [128, 2, 2, D], F32), dma=True)
    gpb = Buf(sb("gpb", [128, 2, D], F32), dma=True)
    gqk_sb = Buf(sb("gqk_sb", [128, 384], F32), dma=True)
    gsub_sb = Buf(sb("gsub_sb", [128, 256], F32), dma=True)
    csc_sb = Buf(sb("csc_sb", [128, 256], F32), dma=True)
    gv_sb = Buf(sb("gv_sb", [128, 256], F32), dma=True)
    bse_sb = Buf(sb("bse_sb", [128, 256], F32), dma=True)
    wst_sb = Buf(sb("wst_sb", [128, 4, 128], BF16), dma=True)
    wc_sb = Buf(sb("wc_sb", [64, 4, 64], BF16), dma=True)
    btm_sb = Buf(sb("btm_sb", [128, 20, 128], BF16), dma=True)
    lq_sb = Buf(sb("lq_sb", [128, 4, L * 32], F32), dma=True)
    lam_sb = Buf(sb("lam_sb", [128, 2 * L + 2 * L], F32))
    small = Rot(20, [16], F32, arena=False, name="small")

    def ps(name, cols):
        return st.enter_context(nc.psum_tensor(name, [128, cols], F32))
    mmp_t = [ps("mmp%d" % i, 1024) for i in range(2)]
    mmb = [Buf(mmp_t[i // 2][:, (i % 2) * 512:(i % 2 + 1) * 512]) for i in range(4)]
    Sp = [Buf(mmp_t[i][:]) for i in range(2)]
    mmi = [0]
    def mmnext():
        b = mmb[mmi[0] % 4]; mmi[0] += 1
        return b
    ptr = Buf(ps("ptr", 1024))
    ptb_t = ps("ptb", 512); ptb = Buf(ptb_t[:].bitcast(BF16))
    pacc = Buf(ps("pacc", 512))

    def R_(bs): return [b.res if isinstance(b, Buf) else b for b in bs]
    def MM(out, lhsT, rhs, start, stop, R, W, **kw):
        fw.op("pe", lambda h: h.matmul(out, lhsT=lhsT, rhs=rhs, start=start, stop=stop, **kw), R_(R), R_(W))
    def TRP(out, in_, idb, R, W):
        fw.op("pe", lambda h: h.transpose(out=out, in_=in_, identity=idb.ap[:]), R_(R) + [idb.res], R_(W))
    def ACT(out, in_, func, R, W, **kw):
        fw.op("act", lambda h: h.activation(out=out, in_=in_, func=func, **kw), R_(R), R_(W))
    def TT(out, in0, in1, op, R, W, eng="dve"):
        fw.op(eng, lambda h: h.tensor_tensor(out=out, in0=in0, in1=in1, op=op), R_(R), R_(W))
    def TS(out, in0, s1, s2, op0, op1, R, W, eng="dve"):
        if op1 is None:
            fw.op(eng, lambda h: h.tensor_scalar(out=out, in0=in0, scalar1=s1, scalar2=None, op0=op0), R_(R), R_(W))
        else:
            fw.op(eng, lambda h: h.tensor_scalar(out=out, in0=in0, scalar1=s1, scalar2=s2, op0=op0, op1=op1), R_(R), R_(W))
    def STT(out, in0, sc, in1, op0, op1, R, W):
        fw.op("dve", lambda h: h.scalar_tensor_tensor(out=out, in0=in0, scalar=sc, in1=in1, op0=op0, op1=op1), R_(R), R_(W))
    def CP(out, in_, R, W, eng="dve"):
        fw.op(eng, lambda h: h.tensor_copy(out=out, in_=in_), R_(R), R_(W))
    def RED(out, in_, R, W):
        fw.op("dve", lambda h: h.tensor_reduce(out=out, in_=in_, op=ALU.add, axis=mybir.AxisListType.X), R_(R), R_(W))
    def RCP(out, in_, R, W):
        fw.op("dve", lambda h: h.reciprocal(out=out, in_=in_), R_(R), R_(W))
    def MSET(ap, val, W, eng="pool"):
        fw.op(eng, lambda h: h.memset(ap, val), [], R_(W))
    poolch = {}
    def DMA(eng, ch, out, in_, R, W, nonc=False):
        if eng == "pool":
            if id(ch) not in poolch:
                poolch[id(ch)] = fw.chan("pq%d" % len(poolch))
            ch = poolch[id(ch)]
        if nonc:
            fw.dma(eng, ch, lambda h: h.dma_start(out=out, in_=in_, allow_slow_non_contiguous=True), R_(R), R_(W))
        else:
            fw.dma(eng, ch, lambda h: h.dma_start(out=out, in_=in_), R_(R), R_(W))
    wch = [fw.chan("w%d" % i) for i in range(24)]; wci = [0]
    def DW(eng, out, in_, R, W):
        c = wch[wci[0] % len(wch)]; wci[0] += 1
        DMA(eng, c, out, in_, R, W)
    rstd_mode = ["pool"]
    def rstd_of(ss_ap, n, b, width=1):
        sbuf = small.next()
        if rstd_mode[0] == "pool":
            TS(sbuf.ap[:, 0:width], ss_ap, 1.0 / n, EPS, ALU.mult, ALU.add, [b], [sbuf], eng="pool")
            TT(sbuf.ap[:, 8:8 + width], sbuf.ap[:, 0:width], mhalf.ap[:, 0:width], ALU.pow, [sbuf, mhalf], [sbuf], eng="pool")
        else:
            ACT(sbuf.ap[:, 0:width], ss_ap, AF.Sqrt, [b, epsb], [sbuf], scale=1.0 / n, bias=epsb.ap[:, 0:1])
            RCP(sbuf.ap[:, 8:8 + width], sbuf.ap[:, 0:width], [sbuf], [sbuf])
        return sbuf, sbuf.ap[:, 8:8 + width]

    mhalf = Buf(sb("mhalf", [128, 8], F32))
    MSET(mhalf.ap[:], -0.5, [mhalf])
    epsb = Buf(sb("epsb", [128, 1], F32))
    MSET(epsb.ap[:], EPS, [epsb])
    MSET(ident.ap[:], 0.0, [ident])
    fw.op("pool", lambda h: h.affine_select(out=ident.ap[:], in_=ident.ap[:], compare_op=ALU.not_equal, fill=1.0,
                                            base=0, pattern=[[-1, 128]], channel_multiplier=1), [ident.res], [ident.res])
    CP(identb.ap[:], ident.ap[:], [ident], [identb])
    DMA("pool", btm_sb.ch, btm_sb.ap[:], btm, [], [btm_sb])
    for s_ in range(2):
        DMA("sp", scT.ch, scT.ap[:, :, s_], cond[s_, :].rearrange("(c p) -> p c", p=128), [], [scT], nonc=True)
    ACT(scT.ap[:], scT.ap[:], AF.Silu, [scT], [scT])
    for i in range(4):
        DMA("sp", lq_sb.ch, lq_sb.ap[:, i, :], lqk[i, :].partition_broadcast(128), [], [lq_sb])
    for (a, b_, o) in ((0, 1, 0), (2, 3, L)):
        tmpb = small.next()
        prod = sb("lprod%d" % a, [128, L * 32], F32)
        pb = Buf(prod)
        TT(pb.ap[:], lq_sb.ap[:, a, :], lq_sb.ap[:, b_, :], ALU.mult, [lq_sb], [pb])
        RED(lam_sb.ap[:, o:o + L], pb.ap[:].rearrange("p (l d) -> p l d", d=32), [pb], [lam_sb])
    ACT(lam_sb.ap[:, 0:2 * L], lam_sb.ap[:, 0:2 * L], AF.Exp, [lam_sb], [lam_sb])
    for l in range(L):
        STT(lam_sb.ap[:, 2 * L + l:2 * L + l + 1], lam_sb.ap[:, L + l:L + l + 1], -LAM_INIT[l], lam_sb.ap[:, l:l + 1],
            ALU.add, ALU.subtract, [lam_sb], [lam_sb])

    def drive(gens, width, stagger):
        it = iter(gens); active = []; since = stagger
        while True:
            if len(active) < width and since >= stagger:
                g = next(it, None)
                if g is not None:
                    active.append(g); since = 0
            if not active:
                break
            for g in list(active):
                try:
                    next(g)
                except StopIteration:
                    active.remove(g)
            since += 1

    def tile_info(t):
        for si, (t0, n, nc_, samp) in enumerate(SEQS):
            if t0 <= t < t0 + n:
                return si, t - t0, n, nc_, samp
    def ktile_of(t):
        si, p, n, ncach, samp = tile_info(t)
        return KT0[si] + ncach // 128 + p

    persist[0] = False
    for l in range(L):
        last = (l == L - 1)
        xsrc = xin if l == 0 else XR
        xdst = y if last else XR
        def modgen(lm, pmbuf=None):
            wm = Rot(2, [8, 512], F32, dma=True); bmR = Rot(2, [512], F32, dma=True); moR = Rot(2, [512], F32)
            for si in range(12):
                cs_ = slice(si * 512, (si + 1) * 512)
                w = wm.next()
                DMA("sp", w.ch, w.ap[:], w_mod[lm].rearrange("(k p) n -> p k n", p=128)[:, :, cs_], [], [w])
                b = bmR.next()
                DMA("sp", b.ch, b.ap[0:2, :], b_mod[lm, cs_].partition_broadcast(2), [], [b])
                pm = pmbuf if pmbuf is not None else mmnext()
                for k in range(8):
                    MM(pm.ap[0:2, :], scT.ap[:, k, :], w.ap[:, k, :], k == 0, k == 7, [scT, w], [pm])
                mo = moR.next()
                TT(mo.ap[0:2, :], pm.ap[0:2, :], b.ap[0:2, :], ALU.add, [pm, b], [mo])
                DW("sp", MODS[lm][:, cs_], mo.ap[0:2, :], [mo], [rs("MODS", lm)])
                yield
        fw.barrier(); areset()
        if l == 0:
            for _ in modgen(0):
                pass
            fw.barrier(); areset()
        wins = [Buf(aget([8, 512], BF16), dma=True) for _ in range(4)]
        for q4 in range(4):
            DMA("pool", wins[q4].ch, wins[q4].ap[:],
                w_in[l].rearrange("(k p) n -> p k n", p=128)[:, :, q4 * 512:(q4 + 1) * 512], [], [wins[q4]])
        for s_ in range(2):
            DMA("sp", modp.ch, modp.ap[:, s_, :, :], MODS[l, s_, :].rearrange("(i c p) -> p i c", p=128, c=8),
                [rs("MODS", l)], [modp], nonc=True)
        DMA("sp", gpre_sb.ch, gpre_sb.ap[:], gpre[l], [], [gpre_sb])
        for w_ in range(2):
            DMA("sp", gpb.ch, gpb.ap[:, w_, :], gpost[l, w_, :].partition_broadcast(128), [], [gpb])
        for s in range(2):
            for w_ in range(2):
                DMA("sp", gg.ch, gg.ap[:, s, w_, :], MODS[l, s, (2 + 3 * w_) * D:(3 + 3 * w_) * D].partition_broadcast(128),
                    [rs("MODS", l)], [gg])
        for s in range(2):
            for w_ in range(2):
                TT(gg.ap[:, s, w_, :], gg.ap[:, s, w_, :], gpb.ap[:, w_, :], ALU.mult, [gg, gpb], [gg], eng="pool")
                STT(gms.ap[:, s, w_, :], modp.ap[:, s, 1 + 3 * w_, :], 1.0, gpre_sb.ap[:, w_, :], ALU.add, ALU.mult,
                    [modp, gpre_sb], [gms])
        DMA("sp", gqk_sb.ch, gqk_sb.ap[:], gqk[l, :].partition_broadcast(128), [], [gqk_sb])
        DMA("sp", gsub_sb.ch, gsub_sb.ap[:], gsub[l, :].partition_broadcast(128), [], [gsub_sb])
        TS(gsub_sb.ap[:], gsub_sb.ap[:], 1.0 - LAM_INIT[l], None, ALU.mult, None, [gsub_sb], [gsub_sb])
        DMA("sp", csc_sb.ch, csc_sb.ap[:], csc[l, :].partition_broadcast(128), [], [csc_sb])
        DMA("sp", gv_sb.ch, gv_sb.ap[:], gv[l, :].partition_broadcast(128), [], [gv_sb])
        DMA("sp", bse_sb.ch, bse_sb.ap[:], bse[l], [], [bse_sb])
        DMA("pool", wst_sb.ch, wst_sb.ap[:], wst[l], [], [wst_sb])
        DMA("pool", wc_sb.ch, wc_sb.ap[:], c_w[l].rearrange("g c e -> c g e"), [], [wc_sb], nonc=True)

        def prenorm_p1(xb, R_x):
            junk = junkBR.next(); ssb = small.next()
            ACT(junk.ap[:], xb_ap(xb), AF.Square, R_x, [junk, ssb], accum_out=ssb.ap[:, 0:1])
            rb, r = rstd_of(ssb.ap[:, 0:1], D, ssb)
            xn = junkR.next()
            ACT(xn.ap[:], xb_ap(xb), AF.Identity, R_x + [rb], [xn], scale=r)
            return xn
        def prenorm_p2(xn, s, w_, hT, hcols):
            for c in range(8):
                TRP(ptr.ap[:, c * 128:(c + 1) * 128], xn.ap[:, c * 128:(c + 1) * 128], ident, [xn], [ptr])
            for c in range(8):
                if c % 2 == 0:
                    ACT(hT[0][:, c, hcols], ptr.ap[:, c * 128:(c + 1) * 128], AF.Identity, [ptr, gms, modp], [hT[1]],
                        scale=gms.ap[:, s, w_, c:c + 1], bias=modp.ap[:, s, 3 * w_, c:c + 1])
                else:
                    TS(hT[0][:, c, hcols], ptr.ap[:, c * 128:(c + 1) * 128], gms.ap[:, s, w_, c:c + 1],
                       modp.ap[:, s, 3 * w_, c:c + 1], ALU.mult, ALU.add, [ptr, gms, modp], [hT[1]])
        def prenorm(xb, s, w_, hT, hcols, R_x):
            prenorm_p2(prenorm_p1(xb, R_x), s, w_, hT, hcols)
        def xb_ap(xb):
            return xb[0]

        rstd_mode[0] = "pool"
        xR = Rot(4, [D], F32, dma=True); junkR = Rot(3, [D], F32); junkBR = Rot(3, [D], BF16)
        hTR = Rot(4, [8, 128], BF16); rtR = Rot(4, [192], F32, dma=True)
        f384 = Rot(9, [384], F32); f512 = Rot(12, [512], F32); b512 = Rot(9, [512], BF16)
        f256 = Rot(15, [256], F32); b256 = Rot(9, [256], BF16); b384 = Rot(7, [384], BF16)
        cst = Rot(2, [384 + 256], BF16, dma=True)
        for ct in range(2):
            c_ = cst.next(); kt = KT0[2] + ct
            DMA("pool", c_.ch, c_.ap[:, 0:128], cak[l, ct * 128:(ct + 1) * 128, :], [], [c_])
            DMA("pool", c_.ch, c_.ap[:, 128:384], cbk[l, ct * 128:(ct + 1) * 128, :], [], [c_])
            for j in range(3):
                TRP(ptb.ap[:, j * 128:(j + 1) * 128], c_.ap[:, j * 128:(j + 1) * 128], identb, [c_], [ptb])
            o = b384.next()
            CP(o.ap[:], ptb.ap[:, 0:384], [ptb], [o])
            DW("sp", KTA[:, kt * 128:(kt + 1) * 128], o.ap[:, 0:128], [o], [rs("KTA", kt)])
            DW("sp", KTB[:, :, kt * 128:(kt + 1) * 128], o.ap[:, 128:384].rearrange("p (c t) -> p c t", c=2), [o], [rs("KTB", kt)])
        def tileA(t):
            xb = xR.next()
            DMA("act", xb.ch, xb.ap[:], xsrc[t * 128:(t + 1) * 128, :], [rs("XR", t)], [xb])
            xn_ = prenorm_p1((xb.ap[:],), [xb])
            yield
            yield
            hb = hTR.next()
            prenorm_p2(xn_, 1 if t >= 4 else 0, 0, (hb.ap, hb.res), slice(0, 128))
            yield
            si, pos, nts, ncach, samp = tile_info(t)
            s = 1 if samp else 0
            kt = ktile_of(t); tok = slice(t * 128, (t + 1) * 128)
            if samp:
                rt = rtR.next()
                DMA("act", rt.ch, rt.ap[:], rope[t - 4], [], [rt])
            prow = slice(t * 128, (t + 1) * 128)
            def mmgroup(gi):
                while not mmfree:
                    yield
                pg_ = mmfree.pop(0)
                for k in range(8):
                    MM(pg_.ap[:], hb.ap[:, k, :], wins[gi].ap[:, k, :], k == 0, k == 7, [hb, wins[gi]], [pg_])
                return pg_
            pg = yield from mmgroup(0)
            sq = f384.next(); ss6 = small.next()
            ACT(sq.ap[:], pg.ap[:, 0:384], AF.Square, [pg], [sq])
            yield
            RED(ss6.ap[:, 0:6], sq.ap[:].rearrange("p (h d) -> p h d", d=64), [sq], [ss6])
            yield
            rb, r6 = rstd_of(ss6.ap[:, 0:6], 64, ss6, width=6)
            yield
            qk = f384.next()
            for hh in range(6):
                STT(qk.ap[:, hh * 64:(hh + 1) * 64], pg.ap[:, hh * 64:(hh + 1) * 64], r6[:, hh:hh + 1],
                    gqk_sb.ap[:, hh * 64:(hh + 1) * 64], ALU.mult, ALU.mult, [pg, rb, gqk_sb], [qk])
                if hh % 2 == 1:
                    yield
            qkb = b384.next()
            if not samp:
                DW("pool", nak[l, prow, :], qk.ap[:, 256:384], [qk], [])
                av32 = f256.next()
                CP(av32.ap[:, 0:128], pg.ap[:, 384:512], [pg], [av32])
                DW("pool", nav[l, prow, :], av32.ap[:, 0:128], [av32], [])
                src = qk
            else:
                t1 = f384.next(); t2 = f512.next()
                TT(t1.ap[:].rearrange("p (h d) -> p h d", d=64), qk.ap[:].rearrange("p (h d) -> p h d", d=64),
                   rt.ap[:, 0:64].unsqueeze(1).broadcast_to([128, 6, 64]), ALU.mult, [qk, rt], [t1])
                yield
                x4 = qk.ap[:].rearrange("p (h f s e) -> p h f s e", f=2, s=2, e=16)
                s4 = rt.ap[:, 64:128].rearrange("p (f s e) -> p f s e", s=2, e=16)
                o4 = t2.ap[:, 0:384].rearrange("p (h f s e) -> p h f s e", f=2, s=2, e=16)
                TT(o4[:, :, :, 0, :], x4[:, :, :, 1, :], s4[:, :, 0, :].unsqueeze(1).broadcast_to([128, 6, 2, 16]), ALU.mult, [qk, rt], [t2])
                TT(o4[:, :, :, 1, :], x4[:, :, :, 0, :], s4[:, :, 1, :].unsqueeze(1).broadcast_to([128, 6, 2, 16]), ALU.mult, [qk, rt], [t2])
                yield
                TT(t1.ap[:], t1.ap[:], t2.ap[:, 0:384], ALU.add, [t1, t2], [t1])
                src = t1
            yield
            CP(qkb.ap[:, 0:256].rearrange("p (g kv d) -> p kv g d", g=2, kv=2),
               src.ap[:, 0:256].rearrange("p (kv g d) -> p kv g d", g=2, kv=2), [src], [qkb])
            CP(qkb.ap[:, 256:384], src.ap[:, 256:384], [src], [qkb], eng="pool")
            yield
            for j in range(3):
                TRP(ptb.ap[:, j * 128:(j + 1) * 128], qkb.ap[:, j * 128:(j + 1) * 128], identb, [qkb], [ptb])
            o = b384.next()
            CP(o.ap[:], ptb.ap[:, 0:384], [ptb], [o])
            DW("sp", QTA[:, :, tok], o.ap[:, 0:256].rearrange("p (c t) -> p c t", c=2), [o], [rs("QTA", t)])
            DW("sp", KTA[:, kt * 128:(kt + 1) * 128], o.ap[:, 256:384], [o], [rs("KTA", kt)])
            vb_ = b256.next()
            CP(vb_.ap[:, 0:128], pg.ap[:, 384:512], [pg], [vb_])
            DW("sp", VA[kt * 128:(kt + 1) * 128, :], vb_.ap[:, 0:128], [vb_], [rs("VA", kt)])
            yield
            mmfree.append(pg)
            pg = yield from mmgroup(1)
            qb = b512.next()
            if not samp:
                CP(qb.ap[:], pg.ap[:], [pg], [qb])
                k32 = f256.next()
                CP(k32.ap[:], pg.ap[:, 256:512], [pg], [k32])
                DW("pool", nbk[l, prow, :], k32.ap[:], [k32], [])
            else:
                t1 = f512.next(); t2 = f512.next()
                TT(t1.ap[:].rearrange("p (h d) -> p h d", d=32), pg.ap[:].rearrange("p (h d) -> p h d", d=32),
                   rt.ap[:, 128:160].unsqueeze(1).broadcast_to([128, 16, 32]), ALU.mult, [pg, rt], [t1])
                yield
                x4 = pg.ap[:].rearrange("p (h f s e) -> p h f s e", f=2, s=2, e=8)
                s4 = rt.ap[:, 160:192].rearrange("p (f s e) -> p f s e", s=2, e=8)
                o4 = t2.ap[:].rearrange("p (h f s e) -> p h f s e", f=2, s=2, e=8)
                TT(o4[:, :, :, 0, :], x4[:, :, :, 1, :], s4[:, :, 0, :].unsqueeze(1).broadcast_to([128, 16, 2, 8]), ALU.mult, [pg, rt], [t2])
                TT(o4[:, :, :, 1, :], x4[:, :, :, 0, :], s4[:, :, 1, :].unsqueeze(1).broadcast_to([128, 16, 2, 8]), ALU.mult, [pg, rt], [t2])
                yield
                TT(qb.ap[:], t1.ap[:], t2.ap[:], ALU.add, [t1, t2], [qb])
            yield
            yield
            for j in range(4):
                TRP(ptb.ap[:, j * 128:(j + 1) * 128], qb.ap[:, j * 128:(j + 1) * 128], identb, [qb], [ptb])
            o = b512.next()
            CP(o.ap[:], ptb.ap[:, 0:512], [ptb], [o])
            DW("sp", QTB[:, :, tok], o.ap[:, 0:256].rearrange("p (c t) -> p c t", c=2), [o], [rs("QTB", t)])
            DW("sp", KTB[:, :, kt * 128:(kt + 1) * 128], o.ap[:, 256:512].rearrange("p (c t) -> p c t", c=2), [o], [rs("KTB", kt)])
            yield
            mmfree.append(pg)
            pg = yield from mmgroup(2)
            vc = b512.next()
            CP(vc.ap[:], pg.ap[:], [pg], [vc])
            DW("sp", VB[kt * 128:(kt + 1) * 128, :], vc.ap[:, 0:256], [vc], [rs("VB", kt)])
            DW("sp", CX[tok, :], vc.ap[:, 256:512], [vc], [rs("CX", t)])
            if not samp:
                v32 = f256.next()
                CP(v32.ap[:], pg.ap[:, 0:256], [pg], [v32])
                DW("pool", nbv[l, prow, :], v32.ap[:], [v32], [])
            yield
            mmfree.append(pg)
            pg = yield from mmgroup(3)
            gz = f512.next()
            ACT(gz.ap[:], pg.ap[:], AF.Gelu_apprx_tanh, [pg], [gz])
            mmfree.append(pg)
            yield
            junk = f256.next(); ssv = small.next()
            ACT(junk.ap[:], gz.ap[:, 256:512], AF.Square, [gz], [junk, ssv], accum_out=ssv.ap[:, 0:1])
            yield
            rb, rv = rstd_of(ssv.ap[:, 0:1], 256, ssv)
            yield
            vn = b256.next()
            STT(vn.ap[:], gz.ap[:, 256:512], rv, gv_sb.ap[:], ALU.mult, ALU.mult, [gz, rb, gv_sb], [vn])
            yield
            for g in range(4):
                MM(pacc.ap[:, g * 64:(g + 1) * 64], wst_sb.ap[:, g, :], vn.ap[:, g * 64:(g + 1) * 64], True, True,
                   [wst_sb, vn], [pacc])
            dt_ = f256.next()
            TT(dt_.ap[:], pacc.ap[:, 0:256], bse_sb.ap[:], ALU.add, [pacc, bse_sb], [dt_])
            db = b256.next()
            TT(db.ap[:], dt_.ap[:], gz.ap[:, 0:256], ALU.mult, [dt_, gz], [db])
            DW("sp", MIX[tok, 768:1024], db.ap[:], [db], [rs("MIXd", t)])

        mmfree = list(mmb)
        drive([tileA(t) for t in range(NT)], 3, 8)

        fw.barrier(); areset()
        ktA = Buf(aget([18 * 128], BF16), dma=True); qtA = Buf(aget([2, 2048], BF16), dma=True)
        v1A = Buf(aget([18, 2, 65], BF16), dma=True)
        ktB = Buf(aget([2, 18 * 128], BF16), dma=True); qtB = Buf(aget([2, 2048], BF16), dma=True)
        v1B = Buf(aget([18, 4, 65], BF16), dma=True)
        ptR = Rot(4, [2, 512], BF16); spi = [0]; oA = Rot(4, [4, 256], BF16); oj = Rot(8, [4, 64], F32)
        accs = [(pacc, Buf(ptr.ap[:, 0:512])), (Buf(ptr.ap[:, 512:1024]), Buf(ptb_t[:]))]
        MSET(v1A.ap[:, :, :, 64:65], 1.0, [v1A]); MSET(v1B.ap[:, :, :, 64:65], 1.0, [v1B])
        for si, (t0, nts, ncach, samp) in enumerate(SEQS):
            nkt = nts + ncach // 128; k0 = KT0[si]; nq = nts * 128
            kres = lambda nm: [rs(nm, k0 + i) for i in range(nkt)]
            DMA("sp", ktA.ch, ktA.ap[:, 0:nkt * 128], KTA[:, k0 * 128:(k0 + nkt) * 128], kres("KTA"), [ktA])
            DMA("sp", ktB.ch, ktB.ap[:, :, 0:nkt * 128], KTB[:, :, k0 * 128:(k0 + nkt) * 128], kres("KTB"), [ktB])
            DMA("sp", qtA.ch, qtA.ap[:, :, 0:nq], QTA[:, :, t0 * 128:t0 * 128 + nq], [rs("QTA", t0 + i) for i in range(nts)], [qtA])
            DMA("sp", qtB.ch, qtB.ap[:, :, 0:nq], QTB[:, :, t0 * 128:t0 * 128 + nq], [rs("QTB", t0 + i) for i in range(nts)], [qtB])
            nct = ncach // 128
            nl = nkt - nct
            for kv_ in range(2):
                cs_ = slice(kv_ * 64, kv_ * 64 + 64)
                if nct:
                    DMA("pool", v1A.ch, v1A.ap[:, 0:nct, kv_, 0:64], cav[l].rearrange("(kt p) c -> p kt c", p=128)[:, :, cs_], [], [v1A])
                DMA("sp", v1A.ch, v1A.ap[:, nct:nkt, kv_, 0:64],
                    VA[(k0 + nct) * 128:(k0 + nkt) * 128, :].rearrange("(kt p) c -> p kt c", p=128)[:, :, cs_],
                    [rs("VA", k0 + nct + i) for i in range(nl)], [v1A])
            for h_ in range(4):
                cs_ = slice(h_ * 64, h_ * 64 + 64)
                if nct:
                    DMA("pool", v1B.ch, v1B.ap[:, 0:nct, h_, 0:64], cbv[l].rearrange("(kt p) c -> p kt c", p=128)[:, :, cs_], [], [v1B])
                DMA("sp", v1B.ch, v1B.ap[:, nct:nkt, h_, 0:64],
                    VB[(k0 + nct) * 128:(k0 + nkt) * 128, :].rearrange("(kt p) c -> p kt c", p=128)[:, :, cs_],
                    [rs("VB", k0 + nct + i) for i in range(nl)], [v1B])
            for q0 in range(0, nq, 512):
                nqc = min(512, nq - q0); nqt = nqc // 128
                oa = oA.next(); ob = oA.next()
                hps = []
                for g in range(2):
                    hds = []
                    for kv in range(2):
                        pp = slice(kv * 64, kv * 64 + 64)
                        hds.append(dict(kt=(lambda kb, pp=pp: ktA.ap[pp, kb * 128:(kb + 1) * 128]),
                                        qt=qtA.ap[pp, g, q0:q0 + nqc], v=(lambda kb, kv=kv: v1A.ap[:, kb, kv, :]),
                                        tp=(kv * 64, 0), h=2 * kv + g))
                    hps.append(dict(kind="A", hds=hds, scale=0.125, R=[ktA, qtA, v1A]))
                for hb_ in range(4):
                    hds = []
                    for j in range(2):
                        sh = 2 * hb_ + j; ch = sh // 4; pb_ = (sh % 4) * 32; pp = slice(pb_, pb_ + 32)
                        hds.append(dict(kt=(lambda kb, pp=pp, ch=ch: ktB.ap[pp, ch, kb * 128:(kb + 1) * 128]),
                                        qt=qtB.ap[pp, ch, q0:q0 + nqc], v=(lambda kb, hb_=hb_: v1B.ap[:, kb, hb_, :]),
                                        tp=(pb_, 0), h=hb_))
                    hps.append(dict(kind="B", hds=hds, scale=32 ** -0.5, R=[ktB, qtB, v1B], h=hb_))
                items = [(pi, kb) for pi in range(len(hps)) for kb in range(nkt)]
                pts = {}
                def issue_qk(pi, kb):
                    hp = hps[pi]
                    pS = Sp[spi[0] % 2]; spi[0] += 1
                    for e in range(2):
                        hd = hp["hds"][e]
                        MM(pS.ap[:, e * 512:e * 512 + nqc], hd["kt"](kb), hd["qt"], True, True, hp["R"], [pS],
                           tile_position=hd["tp"])
                    pt = ptR.next()
                    ACT(pt.ap[:, :, 0:nqc], pS.ap[:].rearrange("p (e n) -> p e n", e=2)[:, :, 0:nqc], AF.Exp, [pS], [pt],
                        scale=hp["scale"])
                    pts[(pi, kb)] = pt
                def issue_pv(pi, kb):
                    hp = hps[pi]; pt = pts.pop((pi, kb)); ac = accs[pi % 2]
                    for e in range(2):
                        hd = hp["hds"][e]; pa = ac[e]
                        for qi in range(nqt):
                            MM(pa.ap[:, qi * 65:(qi + 1) * 65], pt.ap[:, e, qi * 128:(qi + 1) * 128], hd["v"](kb),
                               kb == 0 and qi == 0, kb == nkt - 1, [pt] + hp["R"], [pa], skip_group_check=True)
                    if kb == nkt - 1:
                        finish(pi)
                def normed(pa, dst_of_qi, W):
                    rcb = small.next()
                    RCP(rcb.ap[:, 0:nqt], pa.ap[:, 0:nqt * 65].rearrange("p (q e) -> p q e", e=65)[:, :, 64], [pa], [rcb])
                    for qi in range(nqt):
                        TS(dst_of_qi(qi), pa.ap[:, qi * 65:qi * 65 + 64], rcb.ap[:, qi:qi + 1], None,
                           ALU.mult, None, [pa, rcb], W)
                def finish(pi):
                    hp = hps[pi]; ac = accs[pi % 2]
                    if hp["kind"] == "A":
                        for e in range(2):
                            h = hp["hds"][e]["h"]
                            normed(ac[e], lambda qi, h=h: oa.ap[:, qi, h * 64:(h + 1) * 64], [oa])
                        if pi == 1:
                            DW("sp", MIX[t0 * 128 + q0:t0 * 128 + q0 + nqc, 0:256].rearrange("(q p) c -> p q c", p=128),
                               oa.ap[:, 0:nqt, :], [oa], [rs("MIXa", (t0 * 128 + q0) // 512)])
                        return
                    hb_ = hp["h"]
                    o0 = oj.next(); o1 = oj.next()
                    normed(ac[0], lambda qi: o0.ap[:, qi, :], [o0])
                    normed(ac[1], lambda qi: o1.ap[:, qi, :], [o1])
                    od = oj.next()
                    STT(od.ap[:, 0:nqt, :], o1.ap[:, 0:nqt, :], lam_sb.ap[:, 2 * L + l:2 * L + l + 1], o0.ap[:, 0:nqt, :],
                        ALU.mult, ALU.add, [o0, o1, lam_sb], [od])
                    sq_ = oj.next(); ssb = small.next()
                    TT(sq_.ap[:, 0:nqt, :], od.ap[:, 0:nqt, :], od.ap[:, 0:nqt, :], ALU.mult, [od], [sq_])
                    RED(ssb.ap[:, 0:nqt], sq_.ap[:, 0:nqt, :], [sq_], [ssb])
                    rb, r4 = rstd_of(ssb.ap[:, 0:nqt], 64, ssb, width=nqt)
                    for qi in range(nqt):
                        STT(ob.ap[:, qi, hb_ * 64:(hb_ + 1) * 64], od.ap[:, qi, :], r4[:, qi:qi + 1],
                            gsub_sb.ap[:, hb_ * 64:(hb_ + 1) * 64], ALU.mult, ALU.mult, [od, rb, gsub_sb], [ob])
                    if hb_ == 3:
                        DW("sp", MIX[t0 * 128 + q0:t0 * 128 + q0 + nqc, 256:512].rearrange("(q p) c -> p q c", p=128),
                           ob.ap[:, 0:nqt, :], [ob], [rs("MIXb", (t0 * 128 + q0) // 512)])
                LAG = 1
                for i_ in range(len(items) + LAG):
                    if i_ < len(items):
                        issue_qk(*items[i_])
                    if i_ >= LAG:
                        issue_pv(*items[i_ - LAG])

        fw.barrier(); areset(); rstd_mode[0] = "act"
        GT = 10
        h2T = Buf(aget([8, GT * 128], BF16))
        wo = Buf(aget([8, D], BF16), dma=True)
        for q4 in range(2):
            DMA("pool", wo.ch, wo.ap[:, :, q4 * 512:(q4 + 1) * 512],
                w_out[l].rearrange("(k p) n -> p k n", p=128)[:, :, q4 * 512:(q4 + 1) * 512], [], [wo])
        cxa = Buf(aget([NT, 256], BF16), dma=True); plR = Rot(3, [512], BF16); ycR = Rot(3, [256], BF16)
        for q5 in range(4):
            DMA("sp", cxa.ch, cxa.ap[:, q5 * 5:(q5 + 1) * 5, :],
                CX[q5 * 640:(q5 + 1) * 640, :].rearrange("(t p) c -> p t c", p=128), [rs("CX", q5 * 5 + i) for i in range(5)], [cxa])
        pcs = [pacc, pacc]
        for t in range(NT):
            si, pos, nts, ncach, samp = tile_info(t)
            slots = []
            if pos > 0: slots.append((t - 1, 12))
            slots.append((t, 0 if pos == 0 else (8 if pos == nts - 1 else 4)))
            if pos < nts - 1: slots.append((t + 1, 16))
            pc = mmnext()
            for g in range(4):
                for i_, (tt_, base) in enumerate(slots):
                    MM(pc.ap[0:64, g * 128:(g + 1) * 128], cxa.ap[:, tt_, g * 64:(g + 1) * 64], btm_sb.ap[:, base + g, :],
                       i_ == 0, i_ == len(slots) - 1, [cxa, btm_sb], [pc])
            pl = plR.next()
            CP(pl.ap[0:64, :], pc.ap[0:64, :], [pc], [pl])
            py = pcs[t % 2]
            for g in range(4):
                MM(py.ap[:, g * 64:(g + 1) * 64], pl.ap[0:64, g * 128:(g + 1) * 128], wc_sb.ap[0:64, g, :], True, True,
                   [pl, wc_sb], [py])
            yc = ycR.next()
            TT(yc.ap[:], py.ap[:, 0:256], csc_sb.ap[:], ALU.mult, [py, csc_sb], [yc])
            DW("sp", MIX[t * 128:(t + 1) * 128, 512:768], yc.ap[:], [yc], [rs("MIXc", t)])

        mixR = Rot(4, [D], BF16, dma=True); mixTR = Rot(4, [8, 128], BF16)
        xR = Rot(4, [D], F32, dma=True)
        junkR = Rot(4, [D], F32); junkBR = Rot(6, [D], BF16); tmpR = Rot(6, [512], F32)
        def postnorm_gen(pss, xb_, s, w_):
            junk = junkBR.next(); ssb = small.next()
            for hf in range(2):
                ACT(junk.ap[:, hf * 512:(hf + 1) * 512], pss[hf].ap[:], AF.Square, [pss[hf]], [junk, ssb],
                    accum_out=ssb.ap[:, hf:hf + 1])
            yield
            TT(ssb.ap[:, 2:3], ssb.ap[:, 0:1], ssb.ap[:, 1:2], ALU.add, [ssb], [ssb])
            yield
            rb, r = rstd_of(ssb.ap[:, 2:3], D, ssb)
            yield
            for hf in range(2):
                tm = tmpR.next()
                STT(tm.ap[:], pss[hf].ap[:], r, gg.ap[:, s, w_, hf * 512:(hf + 1) * 512], ALU.mult, ALU.mult,
                    [pss[hf], rb, gg], [tm])
                yield
                TT(xb_.ap[:, hf * 512:(hf + 1) * 512], xb_.ap[:, hf * 512:(hf + 1) * 512], tm.ap[:], ALU.add, [xb_, tm], [xb_])
        def postnorm_update(pss, xb_, s, w_):
            for _ in postnorm_gen(pss, xb_, s, w_):
                pass
        mg = modgen(l + 1, pacc) if l + 1 < L else iter(())
        def tileE1(t):
            tok = slice(t * 128, (t + 1) * 128); s_ = 0 if t < 4 else 1
            next(mg, None)
            mx = mixR.next()
            DMA("sp", mx.ch, mx.ap[:], MIX[tok, :],
                [rs("MIXa", t // 4), rs("MIXb", t // 4), rs("MIXc", t), rs("MIXd", t)], [mx])
            xs_ = xR.next()
            DMA("sp", xs_.ch, xs_.ap[:], xsrc[tok, :], [rs("XR", t)], [xs_])
            yield
            yield
            mT_ = mixTR.next()
            for c in range(8):
                TRP(ptb.ap[:, c * 128:(c + 1) * 128], mx.ap[:, c * 128:(c + 1) * 128], identb, [mx], [ptb])
            CP(mT_.ap[:].rearrange("p c t -> p (c t)"), ptb.ap[:, 0:1024], [ptb], [mT_])
            yield
            yield
            while len(mmfree) < 2:
                yield
            pss = [mmfree.pop(0), mmfree.pop(0)]
            for hf in range(2):
                for k in range(8):
                    MM(pss[hf].ap[:], mT_.ap[:, k, :], wo.ap[:, k, hf * 512:(hf + 1) * 512], k == 0, k == 7,
                       [mT_, wo], [pss[hf]])
            yield
            yield from postnorm_gen(pss, xs_, s_, 0)
            mmfree.extend(pss)
            DW("pool", XR[tok, :], xs_.ap[:], [xs_], [rs("XR", t)])
            yield
            if t < GT:
                xn_ = prenorm_p1((xs_.ap[:],), [xs_])
                yield
                yield
                prenorm_p2(xn_, s_, 1, (h2T.ap, h2T.res), slice(t * 128, (t + 1) * 128))
        mmfree = list(mmb)
        drive([tileE1(t) for t in range(NT)], 3, 6)
        for _ in mg:
            pass

        fw.barrier(); areset()
        h2T = Buf(aget([8, GT * 128], BF16))
        wd = Buf(aget([NJ, D], BF16), dma=True)
        wguR = Rot(3, [2, 8, 128], BF16, dma=True)
        actT = Buf(aget([NJ, GT * 128], BF16))
        xR = Rot(2, [D], F32, dma=True); xR2 = Rot(1, [D], F32, dma=True)
        junkR = Rot(2, [D], F32); junkBR = Rot(2, [D], BF16); tmpR = Rot(2, [512], F32); sgR = Rot(2, [512], F32)
        CH = [(0, 512), (512, 512), (1024, 256)]
        def ffn_up(grp):
            for j in range(NJ):
                if grp == 0 and j == 3:
                    for q4 in range(2):
                        DMA("pool", wd.ch, wd.ap[:, :, q4 * 512:(q4 + 1) * 512],
                            w_down[l].rearrange("(j p) n -> p j n", p=128)[:, :, q4 * 512:(q4 + 1) * 512], [], [wd])
                w = wguR.next()
                DMA("pool", w.ch, w.ap[:, 0, :, :], w_gate[l].rearrange("(k p) n -> p k n", p=128)[:, :, j * 128:(j + 1) * 128], [], [w])
                DMA("pool", w.ch, w.ap[:, 1, :, :], w_up[l].rearrange("(k p) n -> p k n", p=128)[:, :, j * 128:(j + 1) * 128], [], [w])
                for (c0, cn) in CH:
                    pgt = mmnext(); put = mmnext()
                    for k in range(8):
                        MM(pgt.ap[:, 0:cn], w.ap[:, 0, k, :], h2T.ap[:, k, c0:c0 + cn], k == 0, k == 7, [w, h2T], [pgt])
                    for k in range(8):
                        MM(put.ap[:, 0:cn], w.ap[:, 1, k, :], h2T.ap[:, k, c0:c0 + cn], k == 0, k == 7, [w, h2T], [put])
                    sg = sgR.next()
                    ACT(sg.ap[:, 0:cn], pgt.ap[:, 0:cn], AF.Silu, [pgt], [sg])
                    TT(actT.ap[:, j, c0:c0 + cn], sg.ap[:, 0:cn], put.ap[:, 0:cn], ALU.mult, [sg, put], [actT])
        def ffn_down_tile(grp, ti):
            t = grp * GT + ti; tok = slice(t * 128, (t + 1) * 128)
            xs_ = xR.next()
            DMA("sp", xs_.ch, xs_.ap[:], XR[tok, :], [rs("XR", t)], [xs_])
            pss = [mmnext(), mmnext()]
            for hf in range(2):
                for j in range(NJ):
                    MM(pss[hf].ap[:], actT.ap[:, j, ti * 128:(ti + 1) * 128], wd.ap[:, j, hf * 512:(hf + 1) * 512],
                       j == 0, j == NJ - 1, [actT, wd], [pss[hf]])
            postnorm_update(pss, xs_, 0 if t < 4 else 1, 1)
            DW("pool", xdst[tok, :], xs_.ap[:], [xs_], [rs("XR", t)])
        def prenorm_tile_p1(grp, ti):
            t = grp * GT + ti
            xs_ = xR2.next()
            DMA("sp", xs_.ch, xs_.ap[:], XR[t * 128:(t + 1) * 128, :], [rs("XR", t)], [xs_])
            return prenorm_p1((xs_.ap[:],), [xs_])
        ffn_up(0)
        for ti in range(GT):
            xn_ = prenorm_tile_p1(1, ti)
            ffn_down_tile(0, ti)
            t_ = GT + ti
            prenorm_p2(xn_, 0 if t_ < 4 else 1, 1, (h2T.ap, h2T.res), slice(ti * 128, (ti + 1) * 128))
        ffn_up(1)
        for ti in range(GT):
            ffn_down_tile(1, ti)

    stats = fw.emit(nc, st)
    return nc, st, stats


def _host_consts():
    pos = np.arange(2048); row = (pos // 64).astype(np.float64); col = (pos % 64).astype(np.float64)
    def tab(d, nh):
        half = d // 2
        inv = 10000.0 ** (-np.arange(0, half, 2, dtype=np.float64) / half)
        def part(p):
            ang = p[:, None] * inv[None, :]
            ang = np.concatenate([ang, ang], -1)
            sg = np.concatenate([-np.ones(half // 2), np.ones(half // 2)])
            return np.cos(ang), np.sin(ang) * sg[None, :]
        cr, sr = part(row); cc, sc_ = part(col)
        c = np.concatenate([cr, cc], -1); s = np.concatenate([sr, sc_], -1)
        return np.tile(c, (1, nh)), np.tile(s, (1, nh))
    cA, sA = tab(64, 1); cB, sB = tab(32, 1)
    rope = np.concatenate([cA, sA, cB, sB], -1).astype(np.float32).reshape(16, 128, 192)
    def bmat(win, kind):
        S = 128 * 3
        M = np.zeros((128, 128), np.float64)
        if kind == "first": lo_tile = 0
        elif kind == "mid": lo_tile = 1
        else: lo_tile = 2
        for t in range(128):
            tg = lo_tile * 128 + t
            lo = max(tg - win // 2, 0); hi = min(tg + win // 2, S)
            for tp in range(lo, hi):
                if lo_tile * 128 <= tp < lo_tile * 128 + 128:
                    M[tp - lo_tile * 128, t] += 1.0 / (hi - lo)
            M[t, t] -= 1.0
        return M
    def bnb(win, kind):
        M = np.zeros((128, 128), np.float64)
        for t in range(128):
            if kind == "prev":
                for tp in range(t - win // 2, 0):
                    M[128 + tp, t] += 1.0 / win
            else:
                for tp in range(128, t + win // 2):
                    M[tp - 128, t] += 1.0 / win
        return M
    mats = []
    for kind in ("first", "mid", "last"):
        for w in (2, 4, 8, 16):
            mats.append(bmat(w, kind))
    for kind in ("prev", "next"):
        for w in (2, 4, 8, 16):
            mats.append(bnb(w, kind))
    btm = np.stack(mats, 1).astype(np.float32)
    return rope, btm


_CACHE = {}


def kernel(x_prompt, x_sample, cache_a_k, cache_a_v, cache_b_k, cache_b_v, c, c_ctx,
           w_mod, b_mod, g_pre_mix, g_post_mix, g_pre_ffn, g_post_ffn, w_in, w_out,
           a_q_norm, a_k_norm, b_lq1, b_lk1, b_lq2, b_lk2, b_subln, c_w, c_scale,
           d_v_norm, d_ws, d_bs, w_gate, w_up, w_down):
    f = lambda a: np.ascontiguousarray(np.asarray(a, dtype=np.float32))
    if "nc" not in _CACHE:
        _CACHE["nc"] = build()
    nc = _CACHE["nc"][0]
    rope, btm = _host_consts()
    gpre = np.stack([f(g_pre_mix).reshape(L, 8, 128).transpose(0, 2, 1), f(g_pre_ffn).reshape(L, 8, 128).transpose(0, 2, 1)], 2)
    gpost = np.stack([f(g_post_mix), f(g_post_ffn)], 1)
    gqk = np.concatenate([np.tile(f(a_q_norm), (1, 4)), np.tile(f(a_k_norm), (1, 2))], 1)
    lqk = np.stack([f(b_lq1).reshape(-1), f(b_lk1).reshape(-1), f(b_lq2).reshape(-1), f(b_lk2).reshape(-1)], 0)
    gsub = np.tile(f(b_subln), (1, 4))
    wst = f(d_ws).transpose(0, 3, 1, 2)
    bse = np.repeat(f(d_bs).transpose(0, 2, 1)[:, :, :, None], 64, axis=3).reshape(L, 128, 256)
    shared = dict(w_mod=f(w_mod), b_mod=f(b_mod), gpre=f(gpre), gpost=f(gpost), w_in=f(w_in), w_out=f(w_out),
                  gqk=f(gqk), lqk=f(lqk), gsub=f(gsub), c_w=f(c_w), csc=f(c_scale), gv=f(d_v_norm), wst=f(wst), bse=f(bse),
                  w_gate=f(w_gate), w_up=f(w_up), w_down=f(w_down), rope=rope, btm=btm)
    xp = f(x_prompt); xs = f(x_sample)
    in_maps = []
    for core in range(8):
        b = core // 2
        m = dict(shared)
        m["xin"] = np.ascontiguousarray(np.concatenate([xp[2 * core].reshape(256, D), xp[2 * core + 1].reshape(256, D), xs[b]], 0))
        m["cak"] = f(cache_a_k)[b].reshape(L, 256, 128); m["cav"] = f(cache_a_v)[b].reshape(L, 256, 128)
        m["cbk"] = f(cache_b_k)[b].reshape(L, 256, 256); m["cbv"] = f(cache_b_v)[b].reshape(L, 256, 256)
        m["cond"] = np.ascontiguousarray(np.stack([f(c_ctx), f(c)[b]], 0))
        in_maps.append(m)
    res = run_bass_kernel_spmd(nc, in_maps, core_ids=list(range(8)))
    R = res.results
    yp = np.zeros((16, 256, D), np.float32); ys = np.zeros((4, 2048, D), np.float32)
    nak_ = np.zeros((16, L, 256, 2, 64), np.float32); nav_ = np.zeros((16, L, 256, 2, 64), np.float32)
    nbk_ = np.zeros((16, L, 256, 4, 2, 32), np.float32); nbv_ = np.zeros((16, L, 256, 4, 64), np.float32)
    for core in range(8):
        r = R[core]
        yp[2 * core] = r["y"][0:256]; yp[2 * core + 1] = r["y"][256:512]
        if core % 2 == 0:
            ys[core // 2] = r["y"][512:]
        for s in range(2):
            nak_[2 * core + s] = r["nak"][:, s * 256:(s + 1) * 256].reshape(L, 256, 2, 64)
            nav_[2 * core + s] = r["nav"][:, s * 256:(s + 1) * 256].reshape(L, 256, 2, 64)
            nbk_[2 * core + s] = r["nbk"][:, s * 256:(s + 1) * 256].reshape(L, 256, 4, 2, 32)
            nbv_[2 * core + s] = r["nbv"][:, s * 256:(s + 1) * 256].reshape(L, 256, 4, 64)
    return (yp, ys, nak_, nav_, nbk_, nbv_)
```

```python
import numpy as np
import concourse.bass as bass
import concourse.mybir as mybir

F32 = mybir.dt.float32
BF16 = mybir.dt.bfloat16
AF = mybir.ActivationFunctionType
ALU = mybir.AluOpType


class Res:
    __slots__ = ("name", "w", "r")

    def __init__(self, name):
        self.name = name
        self.w = None
        self.r = []


class Node:
    __slots__ = ("eng", "fn", "deps", "signal", "val", "chan", "idx")

    def __init__(self, eng, fn, chan=None):
        self.eng = eng
        self.fn = fn
        self.deps = []
        self.signal = False
        self.val = None
        self.chan = chan
        self.idx = None


class Chan:
    def __init__(self, name):
        self.name = name
        self.sem = None
        self.nodes = []


class FW:
    ENGS = ("pe", "act", "dve", "pool", "sp")

    def __init__(self):
        self.streams = {e: [] for e in self.ENGS}
        self.chans = []
        self.inorder = {"pe"}
        self.last = {}
        self.open_dma = []
        self.bar = []
        self.bar_pending = set()

    def chan(self, name):
        c = Chan(name)
        self.chans.append(c)
        return c

    def barrier(self):
        self.bar = [n for n in self.last.values()] + list(self.open_dma)
        self.open_dma = []
        self.bar_pending = set(self.ENGS)

    def _track(self, node, reads, writes):
        deps = node.deps
        if node.eng in self.bar_pending:
            deps.extend(self.bar)
            self.bar_pending.discard(node.eng)
        self.last[node.eng] = node
        for r in reads:
            if r.w is not None:
                deps.append(r.w)
            r.r.append(node)
        for w in writes:
            if w.w is not None:
                deps.append(w.w)
            deps.extend(w.r)
            w.w = node
            w.r = []

    def op(self, eng, fn, reads=(), writes=()):
        n = Node(eng, fn)
        n.idx = len(self.streams[eng])
        self._track(n, reads, writes)
        self.streams[eng].append(n)
        return n

    def dma(self, eng, chan, fn, reads=(), writes=()):
        n = Node(eng, fn, chan=chan)
        n.idx = len(self.streams[eng])
        n.signal = True
        if chan.nodes:
            n.deps.append(chan.nodes[-1])
        self.open_dma.append(n)
        chan.nodes.append(n)
        n.val = 16 * len(chan.nodes)
        self._track(n, reads, writes)
        self.streams[eng].append(n)
        return n

    def emit(self, nc, stack, final_wait_eng="sp"):
        for e in self.ENGS:
            for n in self.streams[e]:
                for d in n.deps:
                    if d is n:
                        continue
                    if d.chan is None:
                        if d.eng == n.eng and n.chan is None and d.eng in self.inorder:
                            continue
                        d.signal = True
        esem = {}
        for e in self.ENGS:
            cnt = 0
            for n in self.streams[e]:
                if n.chan is None and n.signal:
                    cnt += 1
                    n.val = cnt
            if cnt > 0:
                esem[e] = stack.enter_context(nc.semaphore("sem_" + e))
        for c in self.chans:
            if c.nodes:
                c.sem = stack.enter_context(nc.semaphore("ch_" + c.name))
        handles = {"pe": nc.tensor, "act": nc.scalar, "dve": nc.vector,
                   "pool": nc.gpsimd, "sp": nc.sync}
        stats = {}
        block = stack.enter_context(nc.Block())

        def emit_stream(e, h):
            known = {}
            nw = 0
            for n in self.streams[e]:
                need = {}
                for d in n.deps:
                    if d is n:
                        continue
                    if d.chan is not None:
                        key = ("c", id(d.chan)); sem = d.chan.sem
                    else:
                        if d.eng == n.eng and n.chan is None and d.eng in self.inorder:
                            continue
                        key = ("e", d.eng); sem = esem[d.eng]
                    if d.val > known.get(key, 0) and d.val > need.get(key, (None, 0))[1]:
                        need[key] = (sem, d.val)
                for key, (sem, v) in need.items():
                    h.wait_ge(sem, v)
                    known[key] = v
                    nw += 1
                inst = n.fn(h)
                if n.chan is not None:
                    inst.then_inc(n.chan.sem, 16)
                elif n.signal:
                    inst.then_inc(esem[e], 1)
            if e == final_wait_eng:
                for c in self.chans:
                    if c.nodes:
                        v = 16 * len(c.nodes)
                        if v > known.get(("c", id(c)), 0):
                            h.wait_ge(c.sem, v)
                for e2, s in esem.items():
                    last = max((n.val for n in self.streams[e2] if n.chan is None and n.signal), default=0)
                    if last > known.get(("e", e2), 0) and e2 != e:
                        h.wait_ge(s, last)
            stats[e] = (len(self.streams[e]), nw)

        @block.sync
        def _(sync):
            emit_stream("sp", sync)

        @block.tensor
        def _(t):
            emit_stream("pe", t)

        @block.scalar
        def _(s):
            emit_stream("act", s)

        @block.vector
        def _(v):
            emit_stream("dve", v)

        @block.gpsimd
        def _(g):
            emit_stream("pool", g)

        return stats


import math
from contextlib import ExitStack
from concourse.bass_utils import run_bass_kernel_spmd

L = 4; D = 1024; T = 2560; NT = 20; DFF = 2816; NJ = 22
SEQS = [(0, 2, 0, False), (2, 2, 0, False), (4, 16, 256, True)]
KT0 = [0, 2, 4]
NKT = 22
EPS = 1e-6
LAM_INIT = [0.8 - 0.6 * math.exp(-0.3 * l) for l in range(L)]


def _prod(s):
    r = 1
    for v in s:
        r *= v
    return r


def build():
    nc = bass.Bass("TRN2", target_bir_lowering=False)
    fw = FW()
    st = ExitStack()

    def din(name, shape, dt=F32):
        return nc.dram_tensor(name, list(shape), dt, kind="ExternalInput").ap()

    def dout(name, shape):
        return nc.dram_tensor(name, list(shape), F32, kind="ExternalOutput").ap()

    def dscr(name, shape, dt):
        return nc.dram_tensor(name, list(shape), dt, kind="Internal").ap()

    xin = din("xin", [T, D]); cak = din("cak", [L, 256, 128]); cav = din("cav", [L, 256, 128])
    cbk = din("cbk", [L, 256, 256]); cbv = din("cbv", [L, 256, 256]); cond = din("cond", [2, D])
    w_mod = din("w_mod", [L, D, 6 * D]); b_mod = din("b_mod", [L, 6 * D])
    gpre = din("gpre", [L, 128, 2, 8]); gpost = din("gpost", [L, 2, D])
    w_in = din("w_in", [L, D, 2048]); w_out = din("w_out", [L, D, D])
    gqk = din("gqk", [L, 384]); lqk = din("lqk", [4, L * 32]); gsub = din("gsub", [L, 256])
    c_w = din("c_w", [L, 4, 64, 64]); csc = din("csc", [L, 256]); gv = din("gv", [L, 256])
    wst = din("wst", [L, 128, 4, 128]); bse = din("bse", [L, 128, 256])
    w_gate = din("w_gate", [L, D, DFF]); w_up = din("w_up", [L, D, DFF]); w_down = din("w_down", [L, DFF, D])
    rope = din("rope", [16, 128, 192]); btm = din("btm", [128, 20, 128])
    y = dout("y", [T, D]); nak = dout("nak", [L, 512, 128]); nav = dout("nav", [L, 512, 128])
    nbk = dout("nbk", [L, 512, 256]); nbv = dout("nbv", [L, 512, 256])

    XR = dscr("XR", [T, D], F32); MODS = dscr("MODS", [L, 2, 6 * D], F32)
    QTA = dscr("QTA", [128, 2, T], BF16); KTA = dscr("KTA", [128, NKT * 128], BF16)
    VA = dscr("VA", [NKT * 128, 128], BF16)
    QTB = dscr("QTB", [128, 2, T], BF16); KTB = dscr("KTB", [128, 2, NKT * 128], BF16)
    VB = dscr("VB", [NKT * 128, 256], BF16)
    CX = dscr("CX", [T, 256], BF16); MIX = dscr("MIX", [T, D], BF16)
    RS = {}
    def rs(name, i):
        k = (name, i)
        if k not in RS:
            RS[k] = Res("%s%d" % (name, i))
        return RS[k]

    def sb(name, shape, dt):
        return st.enter_context(nc.sbuf_tensor(name, list(shape), dt))
    ARENA_B = 165 * 1024
    arena_t = sb("arena", [128, ARENA_B // 2], BF16)
    aoff = [0]
    def areset():
        aoff[0] = 0
        chi[0] = 0
    def aget(shape, dt):
        n = _prod(shape); nb = n * (2 if dt == BF16 else 4)
        off = (aoff[0] + 63) // 64 * 64
        aoff[0] = off + nb
        assert aoff[0] <= ARENA_B, (aoff[0], ARENA_B)
        ap = arena_t[:, off // 2:(off + nb) // 2]
        if dt == F32:
            ap = ap.bitcast(F32)
        if len(shape) > 1:
            names = "abcd"[:len(shape)]
            ap = ap.rearrange("p (%s) -> p %s" % (" ".join(names), " ".join(names)),
                              **{names[i]: shape[i] for i in range(len(shape))})
        return ap
    cnt = [0]
    chpool = []; chi = [0]; persist = [True]
    def getch():
        if persist[0]:
            return fw.chan("pc%d" % len(fw.chans))
        if chi[0] >= len(chpool):
            chpool.append(fw.chan("sc%d" % len(chpool)))
        c = chpool[chi[0]]; chi[0] += 1
        return c
    class Buf:
        def __init__(self, ap, name=None, dma=False):
            cnt[0] += 1
            self.ap = ap; self.res = Res(name or ("b%d" % cnt[0]))
            self.ch = getch() if dma else None
    class Rot:
        def __init__(self, n, shape, dt, dma=False, arena=True, name="r"):
            self.b = []
            for i in range(n):
                ap = aget(shape, dt) if arena else sb("%s%d_%d" % (name, cnt[0], i), [128] + list(shape), dt)
                self.b.append(Buf(ap, dma=dma))
            self.i = 0
        def next(self):
            b = self.b[self.i % len(self.b)]; self.i += 1
            return b

    ident = Buf(sb("ident", [128, 128], F32)); identb = Buf(sb("identb", [128, 128], BF16))
    onesb = Buf(sb("onesb", [128, 1], F32))
    scT = Buf(sb("scT", [128, 8, 2], F32), dma=True)
    modp = Buf(sb("modp", [128, 2, 6, 8], F32), dma=True)
    gpre_sb = Buf(sb("gpre_sb", [128, 2, 8], F32), dma=True)
    gms = Buf(sb("gms", [128, 2, 2, 8], F32))
    gg = Buf(sb("gg", [128, 2, 2, D], F32), dma=True)
    gpb = Buf(sb("gpb", [128, 2, D], F32), dma=True)
    gqk_sb = Buf(sb("gqk_sb", [128, 384], F32), dma=True)
    gsub_sb = Buf(sb("gsub_sb", [128, 256], F32), dma=True)
    csc_sb = Buf(sb("csc_sb", [128, 256], F32), dma=True)
    gv_sb = Buf(sb("gv_sb", [128, 256], F32), dma=True)
    bse_sb = Buf(sb("bse_sb", [128, 256], F32), dma=True)
    wst_sb = Buf(sb("wst_sb", [128, 4, 128], BF16), dma=True)
    wc_sb = Buf(sb("wc_sb", [64, 4, 64], BF16), dma=True)
    btm_sb = Buf(sb("btm_sb", [128, 20, 128], BF16), dma=True)
    lq_sb = Buf(sb("lq_sb", [128, 4, L * 32], F32), dma=True)
    lam_sb = Buf(sb("lam_sb", [128, 2 * L + 2 * L], F32))
    small = Rot(20, [16], F32, arena=False, name="small")

    def ps(name, cols):
        return st.enter_context(nc.psum_tensor(name, [128, cols], F32))
    mmp_t = [ps("mmp%d" % i, 1024) for i in range(2)]
    mmb = [Buf(mmp_t[i // 2][:, (i % 2) * 512:(i % 2 + 1) * 512]) for i in range(4)]
    Sp = [Buf(mmp_t[i][:]) for i in range(2)]
    mmi = [0]
    def mmnext():
        b = mmb[mmi[0] % 4]; mmi[0] += 1
        return b
    ptr = Buf(ps("ptr", 1024))
    ptb_t = ps("ptb", 512); ptb = Buf(ptb_t[:].bitcast(BF16))
    pacc = Buf(ps("pacc", 512))

    def R_(bs): return [b.res if isinstance(b, Buf) else b for b in bs]
    def MM(out, lhsT, rhs, start, stop, R, W, **kw):
        fw.op("pe", lambda h: h.matmul(out, lhsT=lhsT, rhs=rhs, start=start, stop=stop, **kw), R_(R), R_(W))
    def TRP(out, in_, idb, R, W):
        fw.op("pe", lambda h: h.transpose(out=out, in_=in_, identity=idb.ap[:]), R_(R) + [idb.res], R_(W))
    def ACT(out, in_, func, R, W, **kw):
        fw.op("act", lambda h: h.activation(out=out, in_=in_, func=func, **kw), R_(R), R_(W))
    def TT(out, in0, in1, op, R, W, eng="dve"):
        fw.op(eng, lambda h: h.tensor_tensor(out=out, in0=in0, in1=in1, op=op), R_(R), R_(W))
    def TS(out, in0, s1, s2, op0, op1, R, W, eng="dve"):
        if op1 is None:
            fw.op(eng, lambda h: h.tensor_scalar(out=out, in0=in0, scalar1=s1, scalar2=None, op0=op0), R_(R), R_(W))
        else:
            fw.op(eng, lambda h: h.tensor_scalar(out=out, in0=in0, scalar1=s1, scalar2=s2, op0=op0, op1=op1), R_(R), R_(W))
    def STT(out, in0, sc, in1, op0, op1, R, W):
        fw.op("dve", lambda h: h.scalar_tensor_tensor(out=out, in0=in0, scalar=sc, in1=in1, op0=op0, op1=op1), R_(R), R_(W))
    def CP(out, in_, R, W, eng="dve"):
        fw.op(eng, lambda h: h.tensor_copy(out=out, in_=in_), R_(R), R_(W))
    def RED(out, in_, R, W):
        fw.op("dve", lambda h: h.tensor_reduce(out=out, in_=in_, op=ALU.add, axis=mybir.AxisListType.X), R_(R), R_(W))
    def RCP(out, in_, R, W):
        fw.op("dve", lambda h: h.reciprocal(out=out, in_=in_), R_(R), R_(W))
    def MSET(ap, val, W, eng="pool"):
        fw.op(eng, lambda h: h.memset(ap, val), [], R_(W))
    poolch = {}
    def DMA(eng, ch, out, in_, R, W, nonc=False):
        if eng == "pool":
            if id(ch) not in poolch:
                poolch[id(ch)] = fw.chan("pq%d" % len(poolch))
            ch = poolch[id(ch)]
        if nonc:
            fw.dma(eng, ch, lambda h: h.dma_start(out=out, in_=in_, allow_slow_non_contiguous=True), R_(R), R_(W))
        else:
            fw.dma(eng, ch, lambda h: h.dma_start(out=out, in_=in_), R_(R), R_(W))
    wch = [fw.chan("w%d" % i) for i in range(24)]; wci = [0]
    def DW(eng, out, in_, R, W):
        c = wch[wci[0] % len(wch)]; wci[0] += 1
        DMA(eng, c, out, in_, R, W)
    rstd_mode = ["pool"]
    def rstd_of(ss_ap, n, b, width=1):
        sbuf = small.next()
        if rstd_mode[0] == "pool":
            TS(sbuf.ap[:, 0:width], ss_ap, 1.0 / n, EPS, ALU.mult, ALU.add, [b], [sbuf], eng="pool")
            TT(sbuf.ap[:, 8:8 + width], sbuf.ap[:, 0:width], mhalf.ap[:, 0:width], ALU.pow, [sbuf, mhalf], [sbuf], eng="pool")
        else:
            ACT(sbuf.ap[:, 0:width], ss_ap, AF.Sqrt, [b, epsb], [sbuf], scale=1.0 / n, bias=epsb.ap[:, 0:1])
            RCP(sbuf.ap[:, 8:8 + width], sbuf.ap[:, 0:width], [sbuf], [sbuf])
        return sbuf, sbuf.ap[:, 8:8 + width]

    mhalf = Buf(sb("mhalf", [128, 8], F32))
    MSET(mhalf.ap[:], -0.5, [mhalf])
    epsb = Buf(sb("epsb", [128, 1], F32))
    MSET(epsb.ap[:], EPS, [epsb])
    MSET(ident.ap[:], 0.0, [ident])
    fw.op("pool", lambda h: h.affine_select(out=ident.ap[:], in_=ident.ap[:], compare_op=ALU.not_equal, fill=1.0,
                                            base=0, pattern=[[-1, 128]], channel_multiplier=1), [ident.res], [ident.res])
    CP(identb.ap[:], ident.ap[:], [ident], [identb])
    DMA("pool", btm_sb.ch, btm_sb.ap[:], btm, [], [btm_sb])
    for s_ in range(2):
        DMA("sp", scT.ch, scT.ap[:, :, s_], cond[s_, :].rearrange("(c p) -> p c", p=128), [], [scT], nonc=True)
    ACT(scT.ap[:], scT.ap[:], AF.Silu, [scT], [scT])
    for i in range(4):
        DMA("sp", lq_sb.ch, lq_sb.ap[:, i, :], lqk[i, :].partition_broadcast(128), [], [lq_sb])
    for (a, b_, o) in ((0, 1, 0), (2, 3, L)):
        tmpb = small.next()
        prod = sb("lprod%d" % a, [128, L * 32], F32)
        pb = Buf(prod)
        TT(pb.ap[:], lq_sb.ap[:, a, :], lq_sb.ap[:, b_, :], ALU.mult, [lq_sb], [pb])
        RED(lam_sb.ap[:, o:o + L], pb.ap[:].rearrange("p (l d) -> p l d", d=32), [pb], [lam_sb])
    ACT(lam_sb.ap[:, 0:2 * L], lam_sb.ap[:, 0:2 * L], AF.Exp, [lam_sb], [lam_sb])
    for l in range(L):
        STT(lam_sb.ap[:, 2 * L + l:2 * L + l + 1], lam_sb.ap[:, L + l:L + l + 1], -LAM_INIT[l], lam_sb.ap[:, l:l + 1],
            ALU.add, ALU.subtract, [lam_sb], [lam_sb])

    def drive(gens, width, stagger):
        it = iter(gens); active = []; since = stagger
        while True:
            if len(active) < width and since >= stagger:
                g = next(it, None)
                if g is not None:
                    active.append(g); since = 0
            if not active:
                break
            for g in list(active):
                try:
                    next(g)
                except StopIteration:
                    active.remove(g)
            since += 1

    def tile_info(t):
        for si, (t0, n, nc_, samp) in enumerate(SEQS):
            if t0 <= t < t0 + n:
                return si, t - t0, n, nc_, samp
    def ktile_of(t):
        si, p, n, ncach, samp = tile_info(t)
        return KT0[si] + ncach // 128 + p

    persist[0] = False
    for l in range(L):
        last = (l == L - 1)
        xsrc = xin if l == 0 else XR
        xdst = y if last else XR
        def modgen(lm, pmbuf=None):
            wm = Rot(2, [8, 512], F32, dma=True); bmR = Rot(2, [512], F32, dma=True); moR = Rot(2, [512], F32)
            for si in range(12):
                cs_ = slice(si * 512, (si + 1) * 512)
                w = wm.next()
                DMA("sp", w.ch, w.ap[:], w_mod[lm].rearrange("(k p) n -> p k n", p=128)[:, :, cs_], [], [w])
                b = bmR.next()
                DMA("sp", b.ch, b.ap[0:2, :], b_mod[lm, cs_].partition_broadcast(2), [], [b])
                pm = pmbuf if pmbuf is not None else mmnext()
                for k in range(8):
                    MM(pm.ap[0:2, :], scT.ap[:, k, :], w.ap[:, k, :], k == 0, k == 7, [scT, w], [pm])
                mo = moR.next()
                TT(mo.ap[0:2, :], pm.ap[0:2, :], b.ap[0:2, :], ALU.add, [pm, b], [mo])
                DW("sp", MODS[lm][:, cs_], mo.ap[0:2, :], [mo], [rs("MODS", lm)])
                yield
        fw.barrier(); areset()
        if l == 0:
            for _ in modgen(0):
                pass
            fw.barrier(); areset()
        wins = [Buf(aget([8, 512], BF16), dma=True) for _ in range(4)]
        for q4 in range(4):
            DMA("pool", wins[q4].ch, wins[q4].ap[:],
                w_in[l].rearrange("(k p) n -> p k n", p=128)[:, :, q4 * 512:(q4 + 1) * 512], [], [wins[q4]])
        for s_ in range(2):
            DMA("sp", modp.ch, modp.ap[:, s_, :, :], MODS[l, s_, :].rearrange("(i c p) -> p i c", p=128, c=8),
                [rs("MODS", l)], [modp], nonc=True)
        DMA("sp", gpre_sb.ch, gpre_sb.ap[:], gpre[l], [], [gpre_sb])
        for w_ in range(2):
            DMA("sp", gpb.ch, gpb.ap[:, w_, :], gpost[l, w_, :].partition_broadcast(128), [], [gpb])
        for s in range(2):
            for w_ in range(2):
                DMA("sp", gg.ch, gg.ap[:, s, w_, :], MODS[l, s, (2 + 3 * w_) * D:(3 + 3 * w_) * D].partition_broadcast(128),
                    [rs("MODS", l)], [gg])
        for s in range(2):
            for w_ in range(2):
                TT(gg.ap[:, s, w_, :], gg.ap[:, s, w_, :], gpb.ap[:, w_, :], ALU.mult, [gg, gpb], [gg], eng="pool")
                STT(gms.ap[:, s, w_, :], modp.ap[:, s, 1 + 3 * w_, :], 1.0, gpre_sb.ap[:, w_, :], ALU.add, ALU.mult,
                    [modp, gpre_sb], [gms])
        DMA("sp", gqk_sb.ch, gqk_sb.ap[:], gqk[l, :].partition_broadcast(128), [], [gqk_sb])
        DMA("sp", gsub_sb.ch, gsub_sb.ap[:], gsub[l, :].partition_broadcast(128), [], [gsub_sb])
        TS(gsub_sb.ap[:], gsub_sb.ap[:], 1.0 - LAM_INIT[l], None, ALU.mult, None, [gsub_sb], [gsub_sb])
        DMA("sp", csc_sb.ch, csc_sb.ap[:], csc[l, :].partition_broadcast(128), [], [csc_sb])
        DMA("sp", gv_sb.ch, gv_sb.ap[:], gv[l, :].partition_broadcast(128), [], [gv_sb])
        DMA("sp", bse_sb.ch, bse_sb.ap[:], bse[l], [], [bse_sb])
        DMA("pool", wst_sb.ch, wst_sb.ap[:], wst[l], [], [wst_sb])
        DMA("pool", wc_sb.ch, wc_sb.ap[:], c_w[l].rearrange("g c e -> c g e"), [], [wc_sb], nonc=True)

        def prenorm_p1(xb, R_x):
            junk = junkBR.next(); ssb = small.next()
            ACT(junk.ap[:], xb_ap(xb), AF.Square, R_x, [junk, ssb], accum_out=ssb.ap[:, 0:1])
            rb, r = rstd_of(ssb.ap[:, 0:1], D, ssb)
            xn = junkR.next()
            ACT(xn.ap[:], xb_ap(xb), AF.Identity, R_x + [rb], [xn], scale=r)
            return xn
        def prenorm_p2(xn, s, w_, hT, hcols):
            for c in range(8):
                TRP(ptr.ap[:, c * 128:(c + 1) * 128], xn.ap[:, c * 128:(c + 1) * 128], ident, [xn], [ptr])
            for c in range(8):
                if c % 2 == 0:
                    ACT(hT[0][:, c, hcols], ptr.ap[:, c * 128:(c + 1) * 128], AF.Identity, [ptr, gms, modp], [hT[1]],
                        scale=gms.ap[:, s, w_, c:c + 1], bias=modp.ap[:, s, 3 * w_, c:c + 1])
                else:
                    TS(hT[0][:, c, hcols], ptr.ap[:, c * 128:(c + 1) * 128], gms.ap[:, s, w_, c:c + 1],
                       modp.ap[:, s, 3 * w_, c:c + 1], ALU.mult, ALU.add, [ptr, gms, modp], [hT[1]])
        def prenorm(xb, s, w_, hT, hcols, R_x):
            prenorm_p2(prenorm_p1(xb, R_x), s, w_, hT, hcols)
        def xb_ap(xb):
            return xb[0]

        rstd_mode[0] = "pool"
        xR = Rot(4, [D], F32, dma=True); junkR = Rot(3, [D], F32); junkBR = Rot(3, [D], BF16)
        hTR = Rot(4, [8, 128], BF16); rtR = Rot(4, [192], F32, dma=True)
        f384 = Rot(9, [384], F32); f512 = Rot(12, [512], F32); b512 = Rot(9, [512], BF16)
        f256 = Rot(15, [256], F32); b256 = Rot(9, [256], BF16); b384 = Rot(7, [384], BF16)
        cst = Rot(2, [384 + 256], BF16, dma=True)
        for ct in range(2):
            c_ = cst.next(); kt = KT0[2] + ct
            DMA("pool", c_.ch, c_.ap[:, 0:128], cak[l, ct * 128:(ct + 1) * 128, :], [], [c_])
            DMA("pool", c_.ch, c_.ap[:, 128:384], cbk[l, ct * 128:(ct + 1) * 128, :], [], [c_])
            for j in range(3):
                TRP(ptb.ap[:, j * 128:(j + 1) * 128], c_.ap[:, j * 128:(j + 1) * 128], identb, [c_], [ptb])
            o = b384.next()
            CP(o.ap[:], ptb.ap[:, 0:384], [ptb], [o])
            DW("sp", KTA[:, kt * 128:(kt + 1) * 128], o.ap[:, 0:128], [o], [rs("KTA", kt)])
            DW("sp", KTB[:, :, kt * 128:(kt + 1) * 128], o.ap[:, 128:384].rearrange("p (c t) -> p c t", c=2), [o], [rs("KTB", kt)])
        def tileA(t):
            xb = xR.next()
            DMA("act", xb.ch, xb.ap[:], xsrc[t * 128:(t + 1) * 128, :], [rs("XR", t)], [xb])
            xn_ = prenorm_p1((xb.ap[:],), [xb])
            yield
            yield
            hb = hTR.next()
            prenorm_p2(xn_, 1 if t >= 4 else 0, 0, (hb.ap, hb.res), slice(0, 128))
            yield
            si, pos, nts, ncach, samp = tile_info(t)
            s = 1 if samp else 0
            kt = ktile_of(t); tok = slice(t * 128, (t + 1) * 128)
            if samp:
                rt = rtR.next()
                DMA("act", rt.ch, rt.ap[:], rope[t - 4], [], [rt])
            prow = slice(t * 128, (t + 1) * 128)
            def mmgroup(gi):
                while not mmfree:
                    yield
                pg_ = mmfree.pop(0)
                for k in range(8):
                    MM(pg_.ap[:], hb.ap[:, k, :], wins[gi].ap[:, k, :], k == 0, k == 7, [hb, wins[gi]], [pg_])
                return pg_
            pg = yield from mmgroup(0)
            sq = f384.next(); ss6 = small.next()
            ACT(sq.ap[:], pg.ap[:, 0:384], AF.Square, [pg], [sq])
            yield
            RED(ss6.ap[:, 0:6], sq.ap[:].rearrange("p (h d) -> p h d", d=64), [sq], [ss6])
            yield
            rb, r6 = rstd_of(ss6.ap[:, 0:6], 64, ss6, width=6)
            yield
            qk = f384.next()
            for hh in range(6):
                STT(qk.ap[:, hh * 64:(hh + 1) * 64], pg.ap[:, hh * 64:(hh + 1) * 64], r6[:, hh:hh + 1],
                    gqk_sb.ap[:, hh * 64:(hh + 1) * 64], ALU.mult, ALU.mult, [pg, rb, gqk_sb], [qk])
                if hh % 2 == 1:
                    yield
            qkb = b384.next()
            if not samp:
                DW("pool", nak[l, prow, :], qk.ap[:, 256:384], [qk], [])
                av32 = f256.next()
                CP(av32.ap[:, 0:128], pg.ap[:, 384:512], [pg], [av32])
                DW("pool", nav[l, prow, :], av32.ap[:, 0:128], [av32], [])
                src = qk
            else:
                t1 = f384.next(); t2 = f512.next()
                TT(t1.ap[:].rearrange("p (h d) -> p h d", d=64), qk.ap[:].rearrange("p (h d) -> p h d", d=64),
                   rt.ap[:, 0:64].unsqueeze(1).broadcast_to([128, 6, 64]), ALU.mult, [qk, rt], [t1])
                yield
                x4 = qk.ap[:].rearrange("p (h f s e) -> p h f s e", f=2, s=2, e=16)
                s4 = rt.ap[:, 64:128].rearrange("p (f s e) -> p f s e", s=2, e=16)
                o4 = t2.ap[:, 0:384].rearrange("p (h f s e) -> p h f s e", f=2, s=2, e=16)
                TT(o4[:, :, :, 0, :], x4[:, :, :, 1, :], s4[:, :, 0, :].unsqueeze(1).broadcast_to([128, 6, 2, 16]), ALU.mult, [qk, rt], [t2])
                TT(o4[:, :, :, 1, :], x4[:, :, :, 0, :], s4[:, :, 1, :].unsqueeze(1).broadcast_to([128, 6, 2, 16]), ALU.mult, [qk, rt], [t2])
                yield
                TT(t1.ap[:], t1.ap[:], t2.ap[:, 0:384], ALU.add, [t1, t2], [t1])
                src = t1
            yield
            CP(qkb.ap[:, 0:256].rearrange("p (g kv d) -> p kv g d", g=2, kv=2),
               src.ap[:, 0:256].rearrange("p (kv g d) -> p kv g d", g=2, kv=2), [src], [qkb])
            CP(qkb.ap[:, 256:384], src.ap[:, 256:384], [src], [qkb], eng="pool")
            yield
            for j in range(3):
                TRP(ptb.ap[:, j * 128:(j + 1) * 128], qkb.ap[:, j * 128:(j + 1) * 128], identb, [qkb], [ptb])
            o = b384.next()
            CP(o.ap[:], ptb.ap[:, 0:384], [ptb], [o])
            DW("sp", QTA[:, :, tok], o.ap[:, 0:256].rearrange("p (c t) -> p c t", c=2), [o], [rs("QTA", t)])
            DW("sp", KTA[:, kt * 128:(kt + 1) * 128], o.ap[:, 256:384], [o], [rs("KTA", kt)])
            vb_ = b256.next()
            CP(vb_.ap[:, 0:128], pg.ap[:, 384:512], [pg], [vb_])
            DW("sp", VA[kt * 128:(kt + 1) * 128, :], vb_.ap[:, 0:128], [vb_], [rs("VA", kt)])
            yield
            mmfree.append(pg)
            pg = yield from mmgroup(1)
            qb = b512.next()
            if not samp:
                CP(qb.ap[:], pg.ap[:], [pg], [qb])
                k32 = f256.next()
                CP(k32.ap[:], pg.ap[:, 256:512], [pg], [k32])
                DW("pool", nbk[l, prow, :], k32.ap[:], [k32], [])
            else:
                t1 = f512.next(); t2 = f512.next()
                TT(t1.ap[:].rearrange("p (h d) -> p h d", d=32), pg.ap[:].rearrange("p (h d) -> p h d", d=32),
                   rt.ap[:, 128:160].unsqueeze(1).broadcast_to([128, 16, 32]), ALU.mult, [pg, rt], [t1])
                yield
                x4 = pg.ap[:].rearrange("p (h f s e) -> p h f s e", f=2, s=2, e=8)
                s4 = rt.ap[:, 160:192].rearrange("p (f s e) -> p f s e", s=2, e=8)
                o4 = t2.ap[:].rearrange("p (h f s e) -> p h f s e", f=2, s=2, e=8)
                TT(o4[:, :, :, 0, :], x4[:, :, :, 1, :], s4[:, :, 0, :].unsqueeze(1).broadcast_to([128, 16, 2, 8]), ALU.mult, [pg, rt], [t2])
                TT(o4[:, :, :, 1, :], x4[:, :, :, 0, :], s4[:, :, 1, :].unsqueeze(1).broadcast_to([128, 16, 2, 8]), ALU.mult, [pg, rt], [t2])
                yield
                TT(qb.ap[:], t1.ap[:], t2.ap[:], ALU.add, [t1, t2], [qb])
            yield
            yield
            for j in range(4):
                TRP(ptb.ap[:, j * 128:(j + 1) * 128], qb.ap[:, j * 128:(j + 1) * 128], identb, [qb], [ptb])
            o = b512.next()
            CP(o.ap[:], ptb.ap[:, 0:512], [ptb], [o])
            DW("sp", QTB[:, :, tok], o.ap[:, 0:256].rearrange("p (c t) -> p c t", c=2), [o], [rs("QTB", t)])
            DW("sp", KTB[:, :, kt * 128:(kt + 1) * 128], o.ap[:, 256:512].rearrange("p (c t) -> p c t", c=2), [o], [rs("KTB", kt)])
            yield
            mmfree.append(pg)
            pg = yield from mmgroup(2)
            vc = b512.next()
            CP(vc.ap[:], pg.ap[:], [pg], [vc])
            DW("sp", VB[kt * 128:(kt + 1) * 128, :], vc.ap[:, 0:256], [vc], [rs("VB", kt)])
            DW("sp", CX[tok, :], vc.ap[:, 256:512], [vc], [rs("CX", t)])
            if not samp:
                v32 = f256.next()
                CP(v32.ap[:], pg.ap[:, 0:256], [pg], [v32])
                DW("pool", nbv[l, prow, :], v32.ap[:], [v32], [])
            yield
            mmfree.append(pg)
            pg = yield from mmgroup(3)
            gz = f512.next()
            ACT(gz.ap[:], pg.ap[:], AF.Gelu_apprx_tanh, [pg], [gz])
            mmfree.append(pg)
            yield
            junk = f256.next(); ssv = small.next()
            ACT(junk.ap[:], gz.ap[:, 256:512], AF.Square, [gz], [junk, ssv], accum_out=ssv.ap[:, 0:1])
            yield
            rb, rv = rstd_of(ssv.ap[:, 0:1], 256, ssv)
            yield
            vn = b256.next()
            STT(vn.ap[:], gz.ap[:, 256:512], rv, gv_sb.ap[:], ALU.mult, ALU.mult, [gz, rb, gv_sb], [vn])
            yield
            for g in range(4):
                MM(pacc.ap[:, g * 64:(g + 1) * 64], wst_sb.ap[:, g, :], vn.ap[:, g * 64:(g + 1) * 64], True, True,
                   [wst_sb, vn], [pacc])
            dt_ = f256.next()
            TT(dt_.ap[:], pacc.ap[:, 0:256], bse_sb.ap[:], ALU.add, [pacc, bse_sb], [dt_])
            db = b256.next()
            TT(db.ap[:], dt_.ap[:], gz.ap[:, 0:256], ALU.mult, [dt_, gz], [db])
            DW("sp", MIX[tok, 768:1024], db.ap[:], [db], [rs("MIXd", t)])

        mmfree = list(mmb)
        drive([tileA(t) for t in range(NT)], 3, 8)

        fw.barrier(); areset()
        ktA = Buf(aget([18 * 128], BF16), dma=True); qtA = Buf(aget([2, 2048], BF16), dma=True)
        v1A = Buf(aget([18, 2, 65], BF16), dma=True)
        ktB = Buf(aget([2, 18 * 128], BF16), dma=True); qtB = Buf(aget([2, 2048], BF16), dma=True)
        v1B = Buf(aget([18, 4, 65], BF16), dma=True)
        ptR = Rot(4, [2, 512], BF16); spi = [0]; oA = Rot(4, [4, 256], BF16); oj = Rot(8, [4, 64], F32)
        accs = [(pacc, Buf(ptr.ap[:, 0:512])), (Buf(ptr.ap[:, 512:1024]), Buf(ptb_t[:]))]
        MSET(v1A.ap[:, :, :, 64:65], 1.0, [v1A]); MSET(v1B.ap[:, :, :, 64:65], 1.0, [v1B])
        for si, (t0, nts, ncach, samp) in enumerate(SEQS):
            nkt = nts + ncach // 128; k0 = KT0[si]; nq = nts * 128
            kres = lambda nm: [rs(nm, k0 + i) for i in range(nkt)]
            DMA("sp", ktA.ch, ktA.ap[:, 0:nkt * 128], KTA[:, k0 * 128:(k0 + nkt) * 128], kres("KTA"), [ktA])
            DMA("sp", ktB.ch, ktB.ap[:, :, 0:nkt * 128], KTB[:, :, k0 * 128:(k0 + nkt) * 128], kres("KTB"), [ktB])
            DMA("sp", qtA.ch, qtA.ap[:, :, 0:nq], QTA[:, :, t0 * 128:t0 * 128 + nq], [rs("QTA", t0 + i) for i in range(nts)], [qtA])
            DMA("sp", qtB.ch, qtB.ap[:, :, 0:nq], QTB[:, :, t0 * 128:t0 * 128 + nq], [rs("QTB", t0 + i) for i in range(nts)], [qtB])
            nct = ncach // 128
            nl = nkt - nct
            for kv_ in range(2):
                cs_ = slice(kv_ * 64, kv_ * 64 + 64)
                if nct:
                    DMA("pool", v1A.ch, v1A.ap[:, 0:nct, kv_, 0:64], cav[l].rearrange("(kt p) c -> p kt c", p=128)[:, :, cs_], [], [v1A])
                DMA("sp", v1A.ch, v1A.ap[:, nct:nkt, kv_, 0:64],
                    VA[(k0 + nct) * 128:(k0 + nkt) * 128, :].rearrange("(kt p) c -> p kt c", p=128)[:, :, cs_],
                    [rs("VA", k0 + nct + i) for i in range(nl)], [v1A])
            for h_ in range(4):
                cs_ = slice(h_ * 64, h_ * 64 + 64)
                if nct:
                    DMA("pool", v1B.ch, v1B.ap[:, 0:nct, h_, 0:64], cbv[l].rearrange("(kt p) c -> p kt c", p=128)[:, :, cs_], [], [v1B])
                DMA("sp", v1B.ch, v1B.ap[:, nct:nkt, h_, 0:64],
                    VB[(k0 + nct) * 128:(k0 + nkt) * 128, :].rearrange("(kt p) c -> p kt c", p=128)[:, :, cs_],
                    [rs("VB", k0 + nct + i) for i in range(nl)], [v1B])
            for q0 in range(0, nq, 512):
                nqc = min(512, nq - q0); nqt = nqc // 128
                oa = oA.next(); ob = oA.next()
                hps = []
                for g in range(2):
                    hds = []
                    for kv in range(2):
                        pp = slice(kv * 64, kv * 64 + 64)
                        hds.append(dict(kt=(lambda kb, pp=pp: ktA.ap[pp, kb * 128:(kb + 1) * 128]),
                                        qt=qtA.ap[pp, g, q0:q0 + nqc], v=(lambda kb, kv=kv: v1A.ap[:, kb, kv, :]),
                                        tp=(kv * 64, 0), h=2 * kv + g))
                    hps.append(dict(kind="A", hds=hds, scale=0.125, R=[ktA, qtA, v1A]))
                for hb_ in range(4):
                    hds = []
                    for j in range(2):
                        sh = 2 * hb_ + j; ch = sh // 4; pb_ = (sh % 4) * 32; pp = slice(pb_, pb_ + 32)
                        hds.append(dict(kt=(lambda kb, pp=pp, ch=ch: ktB.ap[pp, ch, kb * 128:(kb + 1) * 128]),
                                        qt=qtB.ap[pp, ch, q0:q0 + nqc], v=(lambda kb, hb_=hb_: v1B.ap[:, kb, hb_, :]),
                                        tp=(pb_, 0), h=hb_))
                    hps.append(dict(kind="B", hds=hds, scale=32 ** -0.5, R=[ktB, qtB, v1B], h=hb_))
                items = [(pi, kb) for pi in range(len(hps)) for kb in range(nkt)]
                pts = {}
                def issue_qk(pi, kb):
                    hp = hps[pi]
                    pS = Sp[spi[0] % 2]; spi[0] += 1
                    for e in range(2):
                        hd = hp["hds"][e]
                        MM(pS.ap[:, e * 512:e * 512 + nqc], hd["kt"](kb), hd["qt"], True, True, hp["R"], [pS],
                           tile_position=hd["tp"])
                    pt = ptR.next()
                    ACT(pt.ap[:, :, 0:nqc], pS.ap[:].rearrange("p (e n) -> p e n", e=2)[:, :, 0:nqc], AF.Exp, [pS], [pt],
                        scale=hp["scale"])
                    pts[(pi, kb)] = pt
                def issue_pv(pi, kb):
                    hp = hps[pi]; pt = pts.pop((pi, kb)); ac = accs[pi % 2]
                    for e in range(2):
                        hd = hp["hds"][e]; pa = ac[e]
                        for qi in range(nqt):
                            MM(pa.ap[:, qi * 65:(qi + 1) * 65], pt.ap[:, e, qi * 128:(qi + 1) * 128], hd["v"](kb),
                               kb == 0 and qi == 0, kb == nkt - 1, [pt] + hp["R"], [pa], skip_group_check=True)
                    if kb == nkt - 1:
                        finish(pi)
                def normed(pa, dst_of_qi, W):
                    rcb = small.next()
                    RCP(rcb.ap[:, 0:nqt], pa.ap[:, 0:nqt * 65].rearrange("p (q e) -> p q e", e=65)[:, :, 64], [pa], [rcb])
                    for qi in range(nqt):
                        TS(dst_of_qi(qi), pa.ap[:, qi * 65:qi * 65 + 64], rcb.ap[:, qi:qi + 1], None,
                           ALU.mult, None, [pa, rcb], W)
                def finish(pi):
                    hp = hps[pi]; ac = accs[pi % 2]
                    if hp["kind"] == "A":
                        for e in range(2):
                            h = hp["hds"][e]["h"]
                            normed(ac[e], lambda qi, h=h: oa.ap[:, qi, h * 64:(h + 1) * 64], [oa])
                        if pi == 1:
                            DW("sp", MIX[t0 * 128 + q0:t0 * 128 + q0 + nqc, 0:256].rearrange("(q p) c -> p q c", p=128),
                               oa.ap[:, 0:nqt, :], [oa], [rs("MIXa", (t0 * 128 + q0) // 512)])
                        return
                    hb_ = hp["h"]
                    o0 = oj.next(); o1 = oj.next()
                    normed(ac[0], lambda qi: o0.ap[:, qi, :], [o0])
                    normed(ac[1], lambda qi: o1.ap[:, qi, :], [o1])
                    od = oj.next()
                    STT(od.ap[:, 0:nqt, :], o1.ap[:, 0:nqt, :], lam_sb.ap[:, 2 * L + l:2 * L + l + 1], o0.ap[:, 0:nqt, :],
                        ALU.mult, ALU.add, [o0, o1, lam_sb], [od])
                    sq_ = oj.next(); ssb = small.next()
                    TT(sq_.ap[:, 0:nqt, :], od.ap[:, 0:nqt, :], od.ap[:, 0:nqt, :], ALU.mult, [od], [sq_])
                    RED(ssb.ap[:, 0:nqt], sq_.ap[:, 0:nqt, :], [sq_], [ssb])
                    rb, r4 = rstd_of(ssb.ap[:, 0:nqt], 64, ssb, width=nqt)
                    for qi in range(nqt):
                        STT(ob.ap[:, qi, hb_ * 64:(hb_ + 1) * 64], od.ap[:, qi, :], r4[:, qi:qi + 1],
                            gsub_sb.ap[:, hb_ * 64:(hb_ + 1) * 64], ALU.mult, ALU.mult, [od, rb, gsub_sb], [ob])
                    if hb_ == 3:
                        DW("sp", MIX[t0 * 128 + q0:t0 * 128 + q0 + nqc, 256:512].rearrange("(q p) c -> p q c", p=128),
                           ob.ap[:, 0:nqt, :], [ob], [rs("MIXb", (t0 * 128 + q0) // 512)])
                LAG = 1
                for i_ in range(len(items) + LAG):
                    if i_ < len(items):
                        issue_qk(*items[i_])
                    if i_ >= LAG:
                        issue_pv(*items[i_ - LAG])

        fw.barrier(); areset(); rstd_mode[0] = "act"
        GT = 10
        h2T = Buf(aget([8, GT * 128], BF16))
        wo = Buf(aget([8, D], BF16), dma=True)
        for q4 in range(2):
            DMA("pool", wo.ch, wo.ap[:, :, q4 * 512:(q4 + 1) * 512],
                w_out[l].rearrange("(k p) n -> p k n", p=128)[:, :, q4 * 512:(q4 + 1) * 512], [], [wo])
        cxa = Buf(aget([NT, 256], BF16), dma=True); plR = Rot(3, [512], BF16); ycR = Rot(3, [256], BF16)
        for q5 in range(4):
            DMA("sp", cxa.ch, cxa.ap[:, q5 * 5:(q5 + 1) * 5, :],
                CX[q5 * 640:(q5 + 1) * 640, :].rearrange("(t p) c -> p t c", p=128), [rs("CX", q5 * 5 + i) for i in range(5)], [cxa])
        pcs = [pacc, pacc]
        for t in range(NT):
            si, pos, nts, ncach, samp = tile_info(t)
            slots = []
            if pos > 0: slots.append((t - 1, 12))
            slots.append((t, 0 if pos == 0 else (8 if pos == nts - 1 else 4)))
            if pos < nts - 1: slots.append((t + 1, 16))
            pc = mmnext()
            for g in range(4):
                for i_, (tt_, base) in enumerate(slots):
                    MM(pc.ap[0:64, g * 128:(g + 1) * 128], cxa.ap[:, tt_, g * 64:(g + 1) * 64], btm_sb.ap[:, base + g, :],
                       i_ == 0, i_ == len(slots) - 1, [cxa, btm_sb], [pc])
            pl = plR.next()
            CP(pl.ap[0:64, :], pc.ap[0:64, :], [pc], [pl])
            py = pcs[t % 2]
            for g in range(4):
                MM(py.ap[:, g * 64:(g + 1) * 64], pl.ap[0:64, g * 128:(g + 1) * 128], wc_sb.ap[0:64, g, :], True, True,
                   [pl, wc_sb], [py])
            yc = ycR.next()
            TT(yc.ap[:], py.ap[:, 0:256], csc_sb.ap[:], ALU.mult, [py, csc_sb], [yc])
            DW("sp", MIX[t * 128:(t + 1) * 128, 512:768], yc.ap[:], [yc], [rs("MIXc", t)])

        mixR = Rot(3, [D], BF16, dma=True); mixTR = Rot(2, [8, 128], BF16)
        xR = Rot(4, [D], F32, dma=True)
        junkR = Rot(3, [D], F32); junkBR = Rot(3, [D], BF16); tmpR = Rot(3, [512], F32)
        def postnorm_update(pss, xb_, s, w_):
            junk = junkBR.next(); ssb = small.next()
            for hf in range(2):
                ACT(junk.ap[:, hf * 512:(hf + 1) * 512], pss[hf].ap[:], AF.Square, [pss[hf]], [junk, ssb],
                    accum_out=ssb.ap[:, hf:hf + 1])
            TT(ssb.ap[:, 2:3], ssb.ap[:, 0:1], ssb.ap[:, 1:2], ALU.add, [ssb], [ssb])
            rb, r = rstd_of(ssb.ap[:, 2:3], D, ssb)
            for hf in range(2):
                tm = tmpR.next()
                STT(tm.ap[:], pss[hf].ap[:], r, gg.ap[:, s, w_, hf * 512:(hf + 1) * 512], ALU.mult, ALU.mult,
                    [pss[hf], rb, gg], [tm])
                TT(xb_.ap[:, hf * 512:(hf + 1) * 512], xb_.ap[:, hf * 512:(hf + 1) * 512], tm.ap[:], ALU.add, [xb_, tm], [xb_])
        def mixproj(t):
            tok = slice(t * 128, (t + 1) * 128)
            mx = mixR.next()
            DMA("sp", mx.ch, mx.ap[:], MIX[tok, :],
                [rs("MIXa", t // 4), rs("MIXb", t // 4), rs("MIXc", t), rs("MIXd", t)], [mx])
            xs_ = xR.next()
            DMA("sp", xs_.ch, xs_.ap[:], xsrc[tok, :], [rs("XR", t)], [xs_])
            mT_ = mixTR.next()
            for c in range(8):
                TRP(ptb.ap[:, c * 128:(c + 1) * 128], mx.ap[:, c * 128:(c + 1) * 128], identb, [mx], [ptb])
            CP(mT_.ap[:].rearrange("p c t -> p (c t)"), ptb.ap[:, 0:1024], [ptb], [mT_])
            pss = [mmnext(), mmnext()]
            for hf in range(2):
                for k in range(8):
                    MM(pss[hf].ap[:], mT_.ap[:, k, :], wo.ap[:, k, hf * 512:(hf + 1) * 512], k == 0, k == 7,
                       [mT_, wo], [pss[hf]])
            return pss, xs_
        mg = modgen(l + 1, pacc) if l + 1 < L else iter(())
        nxt = mixproj(0)
        pend = None
        for t in range(NT):
            next(mg, None)
            pss, xs_ = nxt
            if t + 1 < NT:
                nxt = mixproj(t + 1)
            postnorm_update(pss, xs_, 0 if t < 4 else 1, 0)
            DW("pool", XR[t * 128:(t + 1) * 128, :], xs_.ap[:], [xs_], [rs("XR", t)])
            if pend is not None:
                prenorm_p2(pend[0], 0 if pend[1] < 4 else 1, 1, (h2T.ap, h2T.res), slice(pend[1] * 128, (pend[1] + 1) * 128))
                pend = None
            if t < GT:
                pend = (prenorm_p1((xs_.ap[:],), [xs_]), t)

        fw.barrier(); areset()
        h2T = Buf(aget([8, GT * 128], BF16))
        wd = Buf(aget([NJ, D], BF16), dma=True)
        wguR = Rot(3, [2, 8, 128], BF16, dma=True)
        actT = Buf(aget([NJ, GT * 128], BF16))
        xR = Rot(2, [D], F32, dma=True); xR2 = Rot(1, [D], F32, dma=True)
        junkR = Rot(2, [D], F32); junkBR = Rot(2, [D], BF16); tmpR = Rot(2, [512], F32); sgR = Rot(2, [512], F32)
        CH = [(0, 512), (512, 512), (1024, 256)]
        pre = {}
        def load_w(j):
            w = wguR.next()
            DMA("pool", w.ch, w.ap[:, 0, :, :], w_gate[l].rearrange("(k p) n -> p k n", p=128)[:, :, j * 128:(j + 1) * 128], [], [w])
            DMA("pool", w.ch, w.ap[:, 1, :, :], w_up[l].rearrange("(k p) n -> p k n", p=128)[:, :, j * 128:(j + 1) * 128], [], [w])
            return w
        def ffn_up(grp):
            for j in range(NJ):
                if grp == 0 and j == 3:
                    for q4 in range(2):
                        DMA("pool", wd.ch, wd.ap[:, :, q4 * 512:(q4 + 1) * 512],
                            w_down[l].rearrange("(j p) n -> p j n", p=128)[:, :, q4 * 512:(q4 + 1) * 512], [], [wd])
                w = pre.pop((grp, j)) if (grp, j) in pre else load_w(j)
                for (c0, cn) in CH:
                    pgt = mmnext(); put = mmnext()
                    for k in range(8):
                        MM(pgt.ap[:, 0:cn], w.ap[:, 0, k, :], h2T.ap[:, k, c0:c0 + cn], k == 0, k == 7, [w, h2T], [pgt])
                    for k in range(8):
                        MM(put.ap[:, 0:cn], w.ap[:, 1, k, :], h2T.ap[:, k, c0:c0 + cn], k == 0, k == 7, [w, h2T], [put])
                    sg = sgR.next()
                    ACT(sg.ap[:, 0:cn], pgt.ap[:, 0:cn], AF.Silu, [pgt], [sg])
                    TT(actT.ap[:, j, c0:c0 + cn], sg.ap[:, 0:cn], put.ap[:, 0:cn], ALU.mult, [sg, put], [actT])
        def ffn_down_tile(grp, ti):
            t = grp * GT + ti; tok = slice(t * 128, (t + 1) * 128)
            xs_ = xR.next()
            DMA("sp", xs_.ch, xs_.ap[:], XR[tok, :], [rs("XR", t)], [xs_])
            pss = [mmnext(), mmnext()]
            for hf in range(2):
                for j in range(NJ):
                    MM(pss[hf].ap[:], actT.ap[:, j, ti * 128:(ti + 1) * 128], wd.ap[:, j, hf * 512:(hf + 1) * 512],
                       j == 0, j == NJ - 1, [actT, wd], [pss[hf]])
            postnorm_update(pss, xs_, 0 if t < 4 else 1, 1)
            DW("pool", xdst[tok, :], xs_.ap[:], [xs_], [rs("XR", t)])
        def prenorm_tile_p1(grp, ti):
            t = grp * GT + ti
            xs_ = xR2.next()
            DMA("sp", xs_.ch, xs_.ap[:], XR[t * 128:(t + 1) * 128, :], [rs("XR", t)], [xs_])
            return prenorm_p1((xs_.ap[:],), [xs_])
        ffn_up(0)
        for j_ in range(3):
            pre[(1, j_)] = load_w(j_)
        for ti in range(GT):
            xn_ = prenorm_tile_p1(1, ti)
            ffn_down_tile(0, ti)
            t_ = GT + ti
            prenorm_p2(xn_, 0 if t_ < 4 else 1, 1, (h2T.ap, h2T.res), slice(ti * 128, (ti + 1) * 128))
        ffn_up(1)
        for ti in range(GT):
            ffn_down_tile(1, ti)

    stats = fw.emit(nc, st)
    return nc, st, stats


def _host_consts():
    pos = np.arange(2048); row = (pos // 64).astype(np.float64); col = (pos % 64).astype(np.float64)
    def tab(d, nh):
        half = d // 2
        inv = 10000.0 ** (-np.arange(0, half, 2, dtype=np.float64) / half)
        def part(p):
            ang = p[:, None] * inv[None, :]
            ang = np.concatenate([ang, ang], -1)
            sg = np.concatenate([-np.ones(half // 2), np.ones(half // 2)])
            return np.cos(ang), np.sin(ang) * sg[None, :]
        cr, sr = part(row); cc, sc_ = part(col)
        c = np.concatenate([cr, cc], -1); s = np.concatenate([sr, sc_], -1)
        return np.tile(c, (1, nh)), np.tile(s, (1, nh))
    cA, sA = tab(64, 1); cB, sB = tab(32, 1)
    rope = np.concatenate([cA, sA, cB, sB], -1).astype(np.float32).reshape(16, 128, 192)
    def bmat(win, kind):
        S = 128 * 3
        M = np.zeros((128, 128), np.float64)
        if kind == "first": lo_tile = 0
        elif kind == "mid": lo_tile = 1
        else: lo_tile = 2
        for t in range(128):
            tg = lo_tile * 128 + t
            lo = max(tg - win // 2, 0); hi = min(tg + win // 2, S)
            for tp in range(lo, hi):
                if lo_tile * 128 <= tp < lo_tile * 128 + 128:
                    M[tp - lo_tile * 128, t] += 1.0 / (hi - lo)
            M[t, t] -= 1.0
        return M
    def bnb(win, kind):
        M = np.zeros((128, 128), np.float64)
        for t in range(128):
            if kind == "prev":
                for tp in range(t - win // 2, 0):
                    M[128 + tp, t] += 1.0 / win
            else:
                for tp in range(128, t + win // 2):
                    M[tp - 128, t] += 1.0 / win
        return M
    mats = []
    for kind in ("first", "mid", "last"):
        for w in (2, 4, 8, 16):
            mats.append(bmat(w, kind))
    for kind in ("prev", "next"):
        for w in (2, 4, 8, 16):
            mats.append(bnb(w, kind))
    btm = np.stack(mats, 1).astype(np.float32)
    return rope, btm


_CACHE = {}


def kernel(x_prompt, x_sample, cache_a_k, cache_a_v, cache_b_k, cache_b_v, c, c_ctx,
           w_mod, b_mod, g_pre_mix, g_post_mix, g_pre_ffn, g_post_ffn, w_in, w_out,
           a_q_norm, a_k_norm, b_lq1, b_lk1, b_lq2, b_lk2, b_subln, c_w, c_scale,
           d_v_norm, d_ws, d_bs, w_gate, w_up, w_down):
    f = lambda a: np.ascontiguousarray(np.asarray(a, dtype=np.float32))
    if "nc" not in _CACHE:
        _CACHE["nc"] = build()
    nc = _CACHE["nc"][0]
    rope, btm = _host_consts()
    gpre = np.stack([f(g_pre_mix).reshape(L, 8, 128).transpose(0, 2, 1), f(g_pre_ffn).reshape(L, 8, 128).transpose(0, 2, 1)], 2)
    gpost = np.stack([f(g_post_mix), f(g_post_ffn)], 1)
    gqk = np.concatenate([np.tile(f(a_q_norm), (1, 4)), np.tile(f(a_k_norm), (1, 2))], 1)
    lqk = np.stack([f(b_lq1).reshape(-1), f(b_lk1).reshape(-1), f(b_lq2).reshape(-1), f(b_lk2).reshape(-1)], 0)
    gsub = np.tile(f(b_subln), (1, 4))
    wst = f(d_ws).transpose(0, 3, 1, 2)
    bse = np.repeat(f(d_bs).transpose(0, 2, 1)[:, :, :, None], 64, axis=3).reshape(L, 128, 256)
    shared = dict(w_mod=f(w_mod), b_mod=f(b_mod), gpre=f(gpre), gpost=f(gpost), w_in=f(w_in), w_out=f(w_out),
                  gqk=f(gqk), lqk=f(lqk), gsub=f(gsub), c_w=f(c_w), csc=f(c_scale), gv=f(d_v_norm), wst=f(wst), bse=f(bse),
                  w_gate=f(w_gate), w_up=f(w_up), w_down=f(w_down), rope=rope, btm=btm)
    xp = f(x_prompt); xs = f(x_sample)
    in_maps = []
    for core in range(8):
        b = core // 2
        m = dict(shared)
        m["xin"] = np.ascontiguousarray(np.concatenate([xp[2 * core].reshape(256, D), xp[2 * core + 1].reshape(256, D), xs[b]], 0))
        m["cak"] = f(cache_a_k)[b].reshape(L, 256, 128); m["cav"] = f(cache_a_v)[b].reshape(L, 256, 128)
        m["cbk"] = f(cache_b_k)[b].reshape(L, 256, 256); m["cbv"] = f(cache_b_v)[b].reshape(L, 256, 256)
        m["cond"] = np.ascontiguousarray(np.stack([f(c_ctx), f(c)[b]], 0))
        in_maps.append(m)
    res = run_bass_kernel_spmd(nc, in_maps, core_ids=list(range(8)))
    R = res.results
    yp = np.zeros((16, 256, D), np.float32); ys = np.zeros((4, 2048, D), np.float32)
    nak_ = np.zeros((16, L, 256, 2, 64), np.float32); nav_ = np.zeros((16, L, 256, 2, 64), np.float32)
    nbk_ = np.zeros((16, L, 256, 4, 2, 32), np.float32); nbv_ = np.zeros((16, L, 256, 4, 64), np.float32)
    for core in range(8):
        r = R[core]
        yp[2 * core] = r["y"][0:256]; yp[2 * core + 1] = r["y"][256:512]
        if core % 2 == 0:
            ys[core // 2] = r["y"][512:]
        for s in range(2):
            nak_[2 * core + s] = r["nak"][:, s * 256:(s + 1) * 256].reshape(L, 256, 2, 64)
            nav_[2 * core + s] = r["nav"][:, s * 256:(s + 1) * 256].reshape(L, 256, 2, 64)
            nbk_[2 * core + s] = r["nbk"][:, s * 256:(s + 1) * 256].reshape(L, 256, 4, 2, 32)
            nbv_[2 * core + s] = r["nbv"][:, s * 256:(s + 1) * 256].reshape(L, 256, 4, 64)
    return (yp, ys, nak_, nav_, nbk_, nbv_)
```
